# Optimizing a Trainium2 kernel written in Bass

```python
import jax, jax.numpy as jnp
from jax import lax
import numpy as np

D_MODEL = 1024
BATCH = 4
SEQ = 8192
DEPTH = 2

CHUNK = 64
N_MIXERS = 2
N_FOX = (DEPTH + 1) // 2
N_SGU = DEPTH // 2
FOX_HEADS = 16
FOX_HEAD_DIM = D_MODEL // FOX_HEADS
Q_BLOCK = 128
SGU_WIDTH = 2 * D_MODEL
SGU_GROUPS = 8
SGU_GROUP_DIM = SGU_WIDTH // SGU_GROUPS
SGU_BLOCK = 128
D_FF = 2816
CONV_WIDTH = 3
EPS = 1e-6

kernel_name = "fox_gmlp_convffn_adaln_hybrid"


def rmsnorm(x, g):
    xf = x.astype(jnp.float32)
    y = xf * lax.rsqrt(jnp.mean(xf * xf, axis=-1, keepdims=True) + EPS)
    return (y * g.astype(jnp.float32)).astype(x.dtype)


def layernorm(x, g, b):
    xf = x.astype(jnp.float32)
    mu = jnp.mean(xf, axis=-1, keepdims=True)
    var = jnp.mean(jnp.square(xf - mu), axis=-1, keepdims=True)
    y = (xf - mu) * lax.rsqrt(var + EPS)
    return (y * g.astype(jnp.float32) + b.astype(jnp.float32)).astype(x.dtype)


def modulate(h, shift, scale):
    return h * (1 + scale[:, None, :]) + shift[:, None, :]


def forgetting_attention(h, w_in, b_f, q_gain, k_gain, w_out):
    B, S, D = h.shape
    H, Dh = FOX_HEADS, FOX_HEAD_DIM
    proj = h @ w_in
    q, k, v, o, fl = jnp.split(proj, [D, 2 * D, 3 * D, 4 * D], axis=-1)
    q = rmsnorm(q.reshape(B, S, H, Dh), q_gain)
    k = rmsnorm(k.reshape(B, S, H, Dh), k_gain)
    v = v.reshape(B, S, H, Dh)
    logf = jax.nn.log_sigmoid((fl + b_f).astype(jnp.float32))
    F = jnp.cumsum(logf, axis=1).transpose(0, 2, 1)
    scale = Dh ** -0.5
    outs = []
    for qb in range(S // Q_BLOCK):
        q0, q1 = qb * Q_BLOCK, (qb + 1) * Q_BLOCK
        logits = jnp.einsum('bqhd,bkhd->bhqk', q[:, q0:q1], k[:, :q1]).astype(jnp.float32) * scale
        decay = F[:, :, q0:q1, None] - F[:, :, None, :q1]
        qpos = jnp.arange(q0, q1)[:, None]
        kpos = jnp.arange(q1)[None, :]
        logits = jnp.where(kpos <= qpos, logits + decay, -jnp.inf)
        p = jax.nn.softmax(logits, axis=-1).astype(v.dtype)
        outs.append(jnp.einsum('bhqk,bkhd->bqhd', p, v[:, :q1]))
    att = jnp.concatenate(outs, axis=1).reshape(B, S, D)
    return (att * jax.nn.sigmoid(o)) @ w_out


def spatial_gating_mlp(h, w_in, b_in, v_gain, v_bias, w_s, b_s, w_out):
    B, S, _ = h.shape
    z = jax.nn.gelu(h @ w_in + b_in)
    u, v = jnp.split(z, 2, axis=-1)
    v = layernorm(v, v_gain, v_bias)
    n = S // SGU_BLOCK
    v = v.reshape(B, n, SGU_BLOCK, SGU_GROUPS, SGU_GROUP_DIM)
    t = jnp.arange(SGU_BLOCK)
    mask = (t[None, :] // CHUNK) <= (t[:, None] // CHUNK)
    ws = jnp.where(mask[None], w_s, 0)
    mixed = jnp.einsum('gts,bnsgc->bntgc', ws, v) + b_s.T[None, None, :, :, None]
    y = u * mixed.reshape(B, S, SGU_WIDTH)
    return y @ w_out


def conv_gated_ffn(h, w_up, conv_w, conv_b, w_down):
    S = h.shape[1]
    a = h @ w_up
    ap = jnp.pad(a, ((0, 0), (CONV_WIDTH - 1, 0), (0, 0)))
    acc = ap[:, 0:S] * conv_w[0]
    for j in range(1, CONV_WIDTH):
        acc = acc + ap[:, j:j + S] * conv_w[j]
    a = acc + conv_b
    g, val = jnp.split(a, 2, axis=-1)
    return (jax.nn.silu(g) * val) @ w_down


def setup_inputs(seed: int = 0) -> dict:
    key = jax.random.key(seed)
    ks = jax.random.split(key, 24)
    D = D_MODEL
    nrm = jax.random.normal
    f32 = jnp.float32
    return {
        "x": nrm(ks[0], (BATCH, SEQ, D), f32),
        "c": nrm(ks[1], (BATCH, D), f32),
        "fox_w_in": nrm(ks[2], (N_FOX, D, 4 * D + FOX_HEADS), f32) * D ** -0.5,
        "fox_b_f": 3.0 + 0.5 * nrm(ks[3], (N_FOX, FOX_HEADS), f32),
        "fox_q_gain": 1.0 + 0.02 * nrm(ks[4], (N_FOX, FOX_HEAD_DIM), f32),
        "fox_k_gain": 1.0 + 0.02 * nrm(ks[5], (N_FOX, FOX_HEAD_DIM), f32),
        "fox_w_out": nrm(ks[6], (N_FOX, D, D), f32) * D ** -0.5,
        "sgu_w_in": nrm(ks[7], (N_SGU, D, 2 * SGU_WIDTH), f32) * D ** -0.5,
        "sgu_b_in": 0.02 * nrm(ks[8], (N_SGU, 2 * SGU_WIDTH), f32),
        "sgu_v_gain": 1.0 + 0.02 * nrm(ks[9], (N_SGU, SGU_WIDTH), f32),
        "sgu_v_bias": 0.02 * nrm(ks[10], (N_SGU, SGU_WIDTH), f32),
        "sgu_w_s": nrm(ks[11], (N_SGU, SGU_GROUPS, SGU_BLOCK, SGU_BLOCK), f32) * SGU_BLOCK ** -0.5,
        "sgu_b_s": 1.0 + 0.1 * nrm(ks[12], (N_SGU, SGU_GROUPS, SGU_BLOCK), f32),
        "sgu_w_out": nrm(ks[13], (N_SGU, SGU_WIDTH, D), f32) * SGU_WIDTH ** -0.5,
        "ffn_w_up": nrm(ks[14], (DEPTH, D, 2 * D_FF), f32) * D ** -0.5,
        "ffn_conv_w": nrm(ks[15], (DEPTH, CONV_WIDTH, 2 * D_FF), f32) * CONV_WIDTH ** -0.5,
        "ffn_conv_b": 0.02 * nrm(ks[16], (DEPTH, 2 * D_FF), f32),
        "ffn_w_down": nrm(ks[17], (DEPTH, D_FF, D), f32) * D_FF ** -0.5,
        "ada_w": nrm(ks[18], (DEPTH, D, 6 * D), f32) * (0.5 * D ** -0.5),
        "ada_b": 0.02 * nrm(ks[19], (DEPTH, 6 * D), f32),
        "norm1_g": 1.0 + 0.02 * nrm(ks[20], (DEPTH, D), f32),
        "norm2_g": 1.0 + 0.02 * nrm(ks[21], (DEPTH, D), f32),
        "final_g": 1.0 + 0.02 * nrm(ks[22], (D,), f32),
    }


def reference(x, c, fox_w_in, fox_b_f, fox_q_gain, fox_k_gain, fox_w_out,
              sgu_w_in, sgu_b_in, sgu_v_gain, sgu_v_bias, sgu_w_s, sgu_b_s, sgu_w_out,
              ffn_w_up, ffn_conv_w, ffn_conv_b, ffn_w_down,
              ada_w, ada_b, norm1_g, norm2_g, final_g):
    c_act = jax.nn.silu(c)
    for i in range(DEPTH):
        mod = c_act @ ada_w[i] + ada_b[i]
        sh1, sc1, g1, sh2, sc2, g2 = jnp.split(mod, 6, axis=-1)
        h = modulate(rmsnorm(x, norm1_g[i]), sh1, sc1)
        j = i // N_MIXERS
        if i % N_MIXERS == 0:
            y = forgetting_attention(h, fox_w_in[j], fox_b_f[j], fox_q_gain[j],
                                     fox_k_gain[j], fox_w_out[j])
        else:
            y = spatial_gating_mlp(h, sgu_w_in[j], sgu_b_in[j], sgu_v_gain[j],
                                   sgu_v_bias[j], sgu_w_s[j], sgu_b_s[j], sgu_w_out[j])
        x = x + g1[:, None, :] * y
        h = modulate(rmsnorm(x, norm2_g[i]), sh2, sc2)
        x = x + g2[:, None, :] * conv_gated_ffn(h, ffn_w_up[i], ffn_conv_w[i],
                                                ffn_conv_b[i], ffn_w_down[i])
    return rmsnorm(x, final_g)
```

```python
import numpy as np
from contextlib import ExitStack
import concourse.bass as bass
import concourse.mybir as mybir
from concourse.bass_utils import run_bass_kernel_spmd

F32 = mybir.dt.float32
BF16 = mybir.dt.bfloat16
AF = mybir.ActivationFunctionType
ALU = mybir.AluOpType

D = 1024
KC = 8
S = 8192
OWN = 4096
HALO = 256
EXT = OWN + HALO
E0 = S - EXT
H = 16
DH = 64
DFF = 2816
NFC = 44
NEG = -30000.0
STRICT = False
EPS = 1e-6
TILES = [(0, 256)] + [(256 + 512 * i, 512) for i in range(8)]


class Eng:
    def __init__(self, kb, h, name):
        self.h = h
        self.name = name
        self.sem = kb.new_sem("e_" + name)
        self.cnt = 0
        self.waited = {}

    def wait(self, deps):
        for (sname, sem), val in deps.items():
            if self.waited.get(sname, 0) < val:
                self.h.wait_ge(sem, val)
                self.waited[sname] = val


class T:
    def __init__(self, kb, h, name):
        self.kb = kb
        self.h = h
        self.name = name
        self.w = {}
        self.r = {}
        self.dsem = {}

    def __getitem__(self, k):
        return self.h[k]


def _merge(dst, src, skip=None):
    for k, v in src.items():
        if skip is not None and k[0] == skip:
            continue
        if dst.get(k, 0) < v:
            dst[k] = v


class KB:
    def __init__(self, taps=()):
        self.nc = bass.Bass("TRN2", target_bir_lowering=False)
        self.taps = set(taps)
        self.nsem = 0
        self.all_tiles = []
        nc = self.nc
        self.PE = Eng(self, nc.tensor, "pe")
        self.ACT = Eng(self, nc.scalar, "act")
        self.DVE = Eng(self, nc.vector, "dve")
        self.POOL = Eng(self, nc.gpsimd, "pool")
        self.SP = Eng(self, nc.sync, "sp")
        self.engs = [self.PE, self.ACT, self.DVE, self.POOL, self.SP]
        self.stack = None
        self.uid = 0
        self.dsems = []
        self.dpool = {"sp": [], "pool": [], "act": []}
        self.stage_owners = []
        self.marks = []

    def new_sem(self, name):
        self.nsem += 1
        return (name, self.nc.semaphore(name).__enter__())

    def dram(self, name, shape, dt, kind="Internal"):
        if kind == "Internal" and name in self.taps:
            kind = "ExternalOutput"
        t = T(self, self.nc.dram_tensor(name, list(shape), dt, kind=kind).ap(), name)
        return t

    def sb(self, name, shape, dt, persist=False):
        self.uid += 1
        g = self.nc.sbuf_tensor(f"{name}_{self.uid}", list(shape), dt)
        h = g.__enter__() if persist or self.stack is None else self.stack.enter_context(g)
        t = T(self, h, name)
        self.all_tiles.append(t)
        return t

    def ps(self, name, shape, dt):
        g = self.nc.psum_tensor(name, list(shape), dt)
        t = T(self, g.__enter__(), name)
        self.all_tiles.append(t)
        return t

    def _deps(self, eng, r, w):
        deps = {}
        for t in r:
            _merge(deps, t.w)
        skip = eng.sem[0] if (not STRICT or eng.name == "pe") else None
        for t in w:
            _merge(deps, t.w, skip=skip)
            _merge(deps, t.r, skip=skip)
        return deps

    def _commit(self, tok, r, w):
        for t in w:
            t.w = {tok[0]: tok[1]}
            t.r = {}
        for t in r:
            if t in w:
                continue
            if t.r.get(tok[0], 0) < tok[1]:
                t.r[tok[0]] = tok[1]

    def op(self, eng, fn, r=(), w=()):
        eng.wait(self._deps(eng, r, w))
        ins = fn(eng.h)
        eng.cnt += 1
        ins.then_inc(eng.sem[1], 1)
        self._commit((eng.sem, eng.cnt), r, w)

    def mm(self, fns, r=(), w=()):
        eng = self.PE
        eng.wait(self._deps(eng, r, w))
        ins = None
        for fn in fns:
            ins = fn(eng.h)
        eng.cnt += 1
        ins.then_inc(eng.sem[1], 1)
        self._commit((eng.sem, eng.cnt), r, w)

    def dma(self, q, out_ap, in_ap, r=(), w=(), owner=None):
        if q.name not in owner.dsem:
            if self.dpool[q.name]:
                ds = self.dpool[q.name].pop()
            else:
                ds = [self.new_sem("dma" + str(self.nsem)), 0]
                self.dsems.append(ds)
            owner.dsem[q.name] = ds
            if self.stack is not None:
                self.stage_owners.append((owner, q.name))
        deps = {}
        for t in r:
            _merge(deps, t.w)
        for t in w:
            _merge(deps, t.w)
            _merge(deps, t.r)
        q.wait(deps)
        ins = q.h.dma_start(out=out_ap, in_=in_ap)
        ds = owner.dsem[q.name]
        ds[1] += 1
        ins.then_inc(ds[0][1], 16)
        self._commit((ds[0], 16 * ds[1]), r, w)

    def barrier(self):
        deps = {}
        for e in self.engs:
            if e.cnt:
                deps[e.sem] = e.cnt
        for ds in self.dsems:
            if ds[1]:
                deps[ds[0]] = 16 * ds[1]
        for e in self.engs:
            e.wait(deps)

    def stage_begin(self):
        self.stack = ExitStack()

    def stage_end(self):
        self.barrier()
        for t, qn in self.stage_owners:
            self.dpool[qn].append(t.dsem.pop(qn))
        self.stage_owners = []
        self.stack.close()
        self.stack = None
        self.marks.append(self.PE.cnt)


def build(taps=(), upto=99):
    kb = KB(taps)
    nc = kb.nc
    PE, ACT, DVE, POOL, SP = kb.PE, kb.ACT, kb.DVE, kb.POOL, kb.SP

    def din(name, shape, dt=F32):
        return kb.dram(name, shape, dt, kind="ExternalInput")

    xv = din("xv", [S, D])
    cvec = din("cvec", [128, KC])
    kpad = din("kpad", [H, S])
    tmask = din("tmask", [128, HALO])
    maskb_d = din("maskb", [128, 128])
    ada_w = din("ada_w", [2, D, 6 * D])
    ada_bT = din("ada_bT", [128, 96])
    ada_bg = din("ada_bg", [128, 4 * D])
    ngT = din("ngT", [128, 4 * KC])
    final_g = din("final_g", [128, D])
    fox_w_in = din("fox_w_in", [D, 4 * D + H])
    fox_bf = din("fox_bf", [H, 1])
    fox_qg = din("fox_qg", [128, 1])
    fox_kg = din("fox_kg", [128, 1])
    fox_w_out = din("fox_w_out", [D, D])
    sgu_w_in = din("sgu_w_in", [D, 4 * D])
    sgu_buT = din("sgu_buT", [128, 16])
    sgu_bv = din("sgu_bv", [1, 2 * D])
    sgu_vg = din("sgu_vg", [128, 2 * D])
    sgu_vb = din("sgu_vb", [128, 2 * D])
    sgu_wsT = din("sgu_wsT", [128, 8, 128])
    sgu_bs2 = din("sgu_bs2", [2, 8 * 128])
    sel2 = din("sel2", [2, 2])
    sgu_w_out = din("sgu_w_out", [2 * D, D])
    ffn_w_up = din("ffn_w_up", [2, D, 2 * DFF])
    ffn_cw = din("ffn_cw", [2, 128, 4, NFC])
    ffn_w_down = din("ffn_w_down", [2, DFF, D])
    out_d = kb.dram("out", [OWN, D], F32, kind="ExternalOutput")

    KT_d = kb.dram("KT_d", [D, S], BF16)
    VS_d = kb.dram("VS_d", [S, D], BF16)
    QT_d = kb.dram("QT_d", [D, EXT], BF16)
    SG_d = kb.dram("SG_d", [D, EXT], BF16)
    FK_d = kb.dram("FK_d", [H, 3, S], BF16)
    CQ_d = kb.dram("CQ_d", [H, 3, EXT], BF16)
    NLF_d = kb.dram("NLF_d", [H, S], F32)
    GB_d = kb.dram("GB_d", [4, 128, D], F32)
    AT_d = kb.dram("AT_d", [D, EXT], BF16)
    XA_d = kb.dram("XA_d", [EXT, D], F32)
    XB_d = kb.dram("XB_d", [EXT, D], F32)
    HT_d = kb.dram("HT_d", [D, EXT], BF16)
    UT_d = kb.dram("UT_d", [DFF, EXT], BF16)
    YT_d = kb.dram("YT_d", [2 * D, EXT], BF16)

    modT = kb.sb("modT", [128, 96], F32, persist=True)
    gmT = kb.sb("gmT", [128, 4 * KC], F32, persist=True)
    ident = kb.sb("ident", [128, 128], F32, persist=True)
    identb = kb.sb("identb", [128, 128], BF16, persist=True)
    onesf = kb.sb("onesf", [128, 128], F32, persist=True)
    cst = kb.sb("cst", [128, 8], F32, persist=True)
    onesb = kb.sb("onesb", [128, 128], BF16, persist=True)

    pbig = kb.ps("pbig", [128, 4096], F32)

    class PV:
        def __init__(self, lo, n, name):
            self.t = T(kb, None, name)
            self.lo = lo
            self.n = n
            kb.all_tiles.append(self.t)

        def ap(self, p0=0, p1=128, c0=0, c1=None):
            c1 = self.n if c1 is None else c1
            return pbig.h[p0:p1, self.lo + c0:self.lo + c1]

    kb.op(POOL, lambda e: e.memset(cst.h[:, 0:1], EPS), w=[cst])
    kb.op(POOL, lambda e: e.memset(cst.h[:, 1:2], 64 * EPS), w=[cst])
    kb.op(POOL, lambda e: e.memset(cst.h[:, 2:3], 1.0), w=[cst])
    kb.op(POOL, lambda e: e.memset(cst.h[:, 3:4], 1e-30), w=[cst])
    kb.op(POOL, lambda e: e.memset(cst.h[:, 4:5], 0.0), w=[cst])
    kb.op(POOL, lambda e: e.memset(onesb.h[:], 1.0), w=[onesb])
    kb.op(POOL, lambda e: e.memset(onesf.h[:], 1.0), w=[onesf])
    kb.op(POOL, lambda e: e.affine_select(out=ident.h[:], in_=onesf.h[:], pattern=[[-1, 128]],
                                          compare_op=ALU.is_equal, fill=0.0, base=0, channel_multiplier=1),
          r=[onesf], w=[ident])
    kb.op(POOL, lambda e: e.tensor_copy(out=identb.h[:], in_=ident.h[:]), r=[ident], w=[identb])

    def stage_adaln():
        kb.stage_begin()
        c_sb = kb.sb("c_sb", [128, KC], F32)
        c_act = kb.sb("c_act", [128, KC], F32)
        crep = kb.sb("crep", [128, KC, 128], F32)
        abT = kb.sb("abT", [128, 96], F32)
        abg = kb.sb("abg", [128, 4 * D], F32)
        ng = kb.sb("ng", [128, 4 * KC], F32)
        wts = [kb.sb(f"adw{i}", [128, KC, 512], F32) for i in range(4)]
        pcol = PV(0, 512, "pcol")
        pg = [PV(512 + 512 * i, 512, f"pg{i}") for i in range(4)]
        gbc = kb.sb("gbc", [128, 4, D], F32)
        kb.dma(SP, c_sb.h[:], cvec.h[:, :], r=[cvec], w=[c_sb], owner=c_sb)
        kb.dma(SP, abT.h[:], ada_bT.h[:, :], r=[ada_bT], w=[abT], owner=abT)
        kb.dma(SP, abg.h[:], ada_bg.h[:, :], r=[ada_bg], w=[abg], owner=abg)
        kb.dma(SP, ng.h[:], ngT.h[:, :], r=[ngT], w=[ng], owner=ng)
        kb.op(ACT, lambda e: e.activation(out=c_act.h[:], in_=c_sb.h[:], func=AF.Silu), r=[c_sb], w=[c_act])
        for k in range(KC):
            kb.op(DVE, lambda e, k=k: e.tensor_copy(out=crep.h[:, k, :], in_=c_act.h[:, k:k + 1].to_broadcast([128, 128])),
                  r=[c_act], w=[crep])
        it = 0
        for l in range(2):
            for rr in range(12):
                wt = wts[it % 4]
                it += 1
                src = ada_w.h[l, :, rr * 512:(rr + 1) * 512].rearrange("(k p) n -> p k n", p=128)
                kb.dma(SP if it % 2 else ACT, wt.h[:], src, r=[ada_w], w=[wt], owner=wt)
                for j in range(4):
                    col = l * 48 + rr * 4 + j
                    kb.mm([lambda e, k=k, j=j, col=col, wt=wt: e.matmul(pcol.ap(c0=col, c1=col + 1), lhsT=wt.h[:, k, j * 128:(j + 1) * 128],
                                                                     rhs=c_act.h[:, k:k + 1], start=(k == 0), stop=(k == KC - 1))
                           for k in range(KC)], r=[wt, c_act], w=[pcol.t])
                gsel = {4: 0, 5: 1, 10: 2, 11: 3}.get(rr)
                if gsel is not None:
                    pgt = pg[gsel]
                    kb.mm([lambda e, k=k, wt=wt, pgt=pgt: e.matmul(pgt.ap(), lhsT=crep.h[:, k, :], rhs=wt.h[:, k, :],
                                                                  start=(k == 0), stop=(k == KC - 1))
                           for k in range(KC)], r=[wt, crep], w=[pgt.t])
            for gi in range(4):
                w_ = gi // 2
                hf = gi % 2
                kb.op(DVE, lambda e, gi=gi, w_=w_, hf=hf, l=l: e.tensor_tensor(
                    out=gbc.h[:, l * 2 + w_, hf * 512:(hf + 1) * 512], in0=pg[gi].ap(),
                    in1=abg.h[:, (l * 2 + w_) * D + hf * 512:(l * 2 + w_) * D + (hf + 1) * 512], op=ALU.add),
                    r=[pg[gi].t, abg], w=[gbc])
        for gi4 in range(4):
            kb.dma(POOL, GB_d.h[gi4], gbc.h[:, gi4, :], r=[gbc], w=[GB_d], owner=gbc)
        kb.op(DVE, lambda e: e.tensor_tensor(out=modT.h[:], in0=pcol.ap(c1=96), in1=abT.h[:], op=ALU.add),
              r=[pcol.t, abT], w=[modT])
        for l in range(2):
            for w_ in range(2):
                sc0 = l * 48 + (1 + 3 * w_) * 8
                kb.op(DVE, lambda e, l=l, w_=w_, sc0=sc0: e.scalar_tensor_tensor(
                    out=gmT.h[:, (l * 2 + w_) * 8:(l * 2 + w_) * 8 + 8], in0=modT.h[:, sc0:sc0 + 8], scalar=1.0,
                    in1=ng.h[:, (l * 2 + w_) * 8:(l * 2 + w_) * 8 + 8], op0=ALU.add, op1=ALU.mult),
                    r=[modT, ng], w=[gmT])
        kb.stage_end()

    def shcol(l, w_, k):
        c = l * 48 + (3 * w_) * 8 + k
        return modT.h[:, c:c + 1]

    def gmcol(l, w_, k):
        c = (l * 2 + w_) * 8 + k
        return gmT.h[:, c:c + 1]

    class NormCtx:
        def __init__(self, tag, need_xn=True, nbuf=2):
            self.junk = kb.sb("junk" + tag, [128, D], BF16)
            self.ss = [kb.sb(f"ss{tag}{i}", [128, 4], F32) for i in range(2)]
            self.rs = [kb.sb(f"rs{tag}{i}", [128, 4], F32) for i in range(2)]
            self.xn = ([kb.sb(f"xn{tag}{i}", [128, 4, D], BF16) for i in range(nbuf)] * (2 // nbuf)) if need_xn else None
            self.it = 0

    def rstd_of(ctx, xt, nb):
        ss = ctx.ss[ctx.it % 2]
        rs = ctx.rs[ctx.it % 2]
        for j in range(nb):
            kb.op(ACT, lambda e, j=j: e.activation(out=ctx.junk.h[:], in_=xt.h[:, j, :], func=AF.Square,
                                                   accum_out=ss.h[:, j:j + 1]), r=[xt], w=[ctx.junk, ss])
        kb.op(ACT, lambda e: e.activation(out=rs.h[:, 0:nb], in_=ss.h[:, 0:nb], func=AF.Ln, scale=1.0 / D, bias=cst.h[:, 0:1]),
              r=[ss, cst], w=[rs])
        kb.op(ACT, lambda e: e.activation(out=rs.h[:, 0:nb], in_=rs.h[:, 0:nb], func=AF.Exp, scale=-0.5), r=[rs], w=[rs])
        return rs

    ptr = [PV(0, 512, "ptr0"), PV(512, 512, "ptr1")]

    def ptr_ap(i, n):
        return pbig.h[:, i * 512:(i + 1) * 512].bitcast(BF16)[:, 0:n]

    def norm_to_hT(ctx, xt, nb, l, w_, hT, evac_flip=[0]):
        rs = rstd_of(ctx, xt, nb)
        xn = ctx.xn[ctx.it % 2]
        ctx.it += 1
        for j in range(nb):
            kb.op(DVE, lambda e, j=j: e.tensor_scalar(out=xn.h[:, j, :], in0=xt.h[:, j, :], scalar1=rs.h[:, j:j + 1], scalar2=None,
                                                      op0=ALU.mult), r=[xt, rs], w=[xn])
        n = nb * 128
        for k in range(KC):
            pi = k % 2
            kb.mm([lambda e, j=j, k=k, pi=pi: e.transpose(ptr_ap(pi, 512)[:, j * 128:(j + 1) * 128], xn.h[:, j, k * 128:(k + 1) * 128], identb.h[:])
                   for j in range(nb)], r=[xn, identb], w=[ptr[pi].t])
            if evac_flip[0] % 2 == 0:
                kb.op(ACT, lambda e, k=k, pi=pi: e.activation(out=hT.h[:, k, 0:n], in_=ptr_ap(pi, n), func=AF.Identity,
                                                              scale=gmcol(l, w_, k), bias=shcol(l, w_, k)),
                      r=[ptr[pi].t, gmT, modT], w=[hT])
            else:
                kb.op(DVE, lambda e, k=k, pi=pi: e.tensor_scalar(out=hT.h[:, k, 0:n], in0=ptr_ap(pi, n), scalar1=gmcol(l, w_, k),
                                                                 scalar2=shcol(l, w_, k), op0=ALU.mult, op1=ALU.add),
                      r=[ptr[pi].t, gmT, modT], w=[hT])
            evac_flip[0] += 1

    def load_w_bf16(name, src_ap, kchunks, ncols, src_t=None):
        wt = kb.sb(name, [128, kchunks, ncols], BF16)
        for k in range(kchunks):
            if src_t is None:
                kb.dma(POOL, wt.h[:, k, :], src_ap[k * 128:(k + 1) * 128, :], r=[], w=[wt], owner=wt)
            else:
                kb.dma(SP if k % 2 == 0 else ACT, wt.h[:, k, :], src_ap[k * 128:(k + 1) * 128, :], r=[src_t], w=[wt], owner=wt)
        return wt

    WB = {
        "out": kb.dram("WB_out", [D, D], BF16),
        "up0": kb.dram("WB_up0", [D, 2 * DFF], BF16),
        "up1": kb.dram("WB_up1", [D, 2 * DFF], BF16),
        "dn0": kb.dram("WB_dn0", [DFF, D], BF16),
        "dn1": kb.dram("WB_dn1", [DFF, D], BF16),
        "sgi": kb.dram("WB_sgi", [D, 4 * D], BF16),
        "sgo": kb.dram("WB_sgo", [2 * D, D], BF16),
    }
    WSRC = {"out": fox_w_out.h, "up0": ffn_w_up.h[0], "up1": ffn_w_up.h[1], "dn0": ffn_w_down.h[0], "dn1": ffn_w_down.h[1],
            "sgi": sgu_w_in.h, "sgo": sgu_w_out.h}

    def prefetch_w(key, kchunks, ncols, side, cast_src=None, defer=False):
        kb.uid += 1
        g = nc.sbuf_tensor(f"W{key}_{kb.uid}", [128, kchunks, ncols], BF16, side=side)
        wt = T(kb, g.__enter__(), "W" + key)
        kb.all_tiles.append(wt)
        def issue():
            for k in range(kchunks):
                if cast_src is not None:
                    kb.dma(POOL, wt.h[:, k, :], cast_src[k * 128:(k + 1) * 128, :], r=[], w=[wt], owner=wt)
                else:
                    kb.dma(SP if k % 2 == 0 else ACT, wt.h[:, k, :], WB[key].h[k * 128:(k + 1) * 128, :], r=[WB[key]], w=[wt], owner=wt)
        if defer:
            return (wt, g, issue)
        issue()
        return (wt, g, None)

    def free_w(slot):
        slot[1].__exit__(None, None, None)

    def conversion_jobs():
        jobs = []
        for key in ("up0", "dn0", "sgi", "sgo", "up1", "dn1"):
            dst = WB[key]
            src = WSRC[key]
            rows = dst.h.shape[0]
            for r0 in range(0, rows, 256):
                r1 = min(rows, r0 + 256)
                jobs.append(lambda dst=dst, src=src, r0=r0, r1=r1: kb.dma(POOL, dst.h[r0:r1, :], src[r0:r1, :], r=[], w=[dst], owner=dst))
        return jobs

    def stage_proj0(w_in):
        kb.stage_begin()
        ctx = NormCtx("p")
        xts = [kb.sb(f"xt{i}", [128, 4, D], F32) for i in range(2)]
        hTs = [kb.sb(f"hT{i}", [128, KC, 512], BF16) for i in range(2)]
        nlt = [kb.sb(f"nlt{i}", [H, 512], F32) for i in range(2)]
        bd = kb.sb("bd", [128, 128], BF16)
        qg8 = kb.sb("qg8", [128, 1], F32)
        kg1 = kb.sb("kg1", [128, 1], F32)
        nbf = kb.sb("nbf", [H, 1], F32)
        sq = [kb.sb(f"sq{i}", [128, 512], BF16) for i in range(2)]
        rq = [kb.sb(f"rq{i}", [128, 512], F32) for i in range(2)]
        kq = [kb.sb(f"kq{i}", [128, 512], BF16) for i in range(3)]
        vt = [kb.sb(f"vt{i}", [128, 4, D], BF16) for i in range(2)]
        ef = [kb.sb(f"ef{i}", [H, 512], F32) for i in range(2)]
        pmm = [PV(1024 + 512 * i, 512, f"pmm{i}") for i in range(4)]
        pss = [PV(3072 + 512 * i, 512, f"pss{i}") for i in range(2)]
        kb.op(POOL, lambda e: e.memset(bd.h[:], 0.0), w=[bd])
        kb.op(POOL, lambda e: e.memset(bd.h[0:64, 0:64], 1.0), w=[bd])
        kb.op(POOL, lambda e: e.memset(bd.h[64:128, 64:128], 1.0), w=[bd])
        kb.dma(SP, qg8.h[:], fox_qg.h[:, :], r=[fox_qg], w=[qg8], owner=qg8)
        kb.dma(SP, kg1.h[:], fox_kg.h[:, :], r=[fox_kg], w=[kg1], owner=kg1)
        kb.dma(SP, nbf.h[:], fox_bf.h[:, :], r=[fox_bf], w=[nbf], owner=nbf)
        kb.op(DVE, lambda e: e.tensor_scalar(out=qg8.h[:], in0=qg8.h[:], scalar1=8.0, scalar2=None, op0=ALU.mult), r=[qg8], w=[qg8])
        kb.op(DVE, lambda e: e.tensor_scalar(out=nbf.h[:], in0=nbf.h[:], scalar1=-1.0, scalar2=None, op0=ALU.mult), r=[nbf], w=[nbf])
        pmi = [0]
        late = [None]

        def norm_tail(p, sqt, rqt, kqt, pst, kind, hp, tt):
            kb.mm([lambda e: e.matmul(pst.ap(), lhsT=bd.h[:], rhs=sqt.h[:], start=True, stop=True)], r=[bd, sqt], w=[pst.t])
            kb.op(ACT, lambda e: e.activation(out=rqt.h[:], in_=pst.ap(), func=AF.Ln, bias=cst.h[:, 1:2]), r=[pst.t, cst], w=[rqt])
            kb.op(ACT, lambda e: e.activation(out=rqt.h[:], in_=rqt.h[:], func=AF.Exp, scale=-0.5), r=[rqt], w=[rqt])
            gsc = kg1 if kind == "k" else qg8
            kb.op(DVE, lambda e: e.scalar_tensor_tensor(out=kqt.h[:], in0=p.ap(), scalar=gsc.h[:, 0:1], in1=rqt.h[:], op0=ALU.mult, op1=ALU.mult),
                  r=[p.t, rqt, gsc], w=[kqt])
            if kind == "k":
                kb.dma(POOL, KT_d.h[hp * 128:(hp + 1) * 128, tt * 512:(tt + 1) * 512], kqt.h[:], r=[kqt], w=[KT_d], owner=kqt)
            elif tt == 7:
                kb.dma(POOL, QT_d.h[hp * 128:(hp + 1) * 128, 0:256], kqt.h[:, 256:512], r=[kqt], w=[QT_d], owner=kqt)
            else:
                q0 = 256 + (tt - 8) * 512
                kb.dma(POOL, QT_d.h[hp * 128:(hp + 1) * 128, q0:q0 + 512], kqt.h[:], r=[kqt], w=[QT_d], owner=kqt)

        def next_pmm():
            p = pmm[pmi[0] % 4]
            pmi[0] += 1
            return p
        cnt = [0]
        for tt in range(16):
            xt = xts[tt % 2]
            hT = hTs[tt % 2]
            own = tt >= 7
            kb.dma(SP, xt.h[:], xv.h[tt * 512:(tt + 1) * 512, :].rearrange("(j p) d -> p j d", p=128), r=[xv], w=[xt], owner=xt)
            norm_to_hT(ctx, xt, 4, 0, 0, hT)
            jobs = [("k", hp) for hp in range(8)] + ([("q", hp) for hp in range(8)] if own else [])
            for kind, hp in jobs:
                c0 = (D if kind == "k" else 0) + hp * 128
                p = next_pmm()
                kb.mm([lambda e, k=k, c0=c0, p=p: e.matmul(p.ap(), lhsT=w_in.h[:, k, c0:c0 + 128], rhs=hT.h[:, k, :],
                                                         start=(k == 0), stop=(k == KC - 1)) for k in range(KC)],
                      r=[w_in, hT], w=[p.t])
                i = cnt[0]
                cnt[0] += 1
                sqt = sq[i % 2]
                rqt = rq[i % 2]
                kqt = kq[i % 3]
                pst = pss[i % 2]
                kb.op(ACT, lambda e, p=p, sqt=sqt: e.activation(out=sqt.h[:], in_=p.ap(), func=AF.Square), r=[p.t], w=[sqt])
                if late[0] is not None:
                    late[0]()
                late[0] = (lambda p=p, sqt=sqt, rqt=rqt, kqt=kqt, pst=pst, kind=kind, hp=hp, tt=tt: norm_tail(p, sqt, rqt, kqt, pst, kind, hp, tt))
            if late[0] is not None:
                late[0]()
                late[0] = None
            vtt = vt[tt % 2]
            for j in range(4):
                for hf in range(2):
                    p = next_pmm()
                    kb.mm([lambda e, k=k, j=j, hf=hf, p=p: e.matmul(p.ap(), lhsT=hT.h[:, k, j * 128:(j + 1) * 128],
                                                                   rhs=w_in.h[:, k, 2 * D + hf * 512:2 * D + (hf + 1) * 512],
                                                                   start=(k == 0), stop=(k == KC - 1)) for k in range(KC)],
                          r=[w_in, hT], w=[p.t])
                    if (j + hf) % 2 == 0:
                        kb.op(DVE, lambda e, j=j, hf=hf, p=p: e.tensor_copy(out=vtt.h[:, j, hf * 512:(hf + 1) * 512], in_=p.ap()),
                              r=[p.t], w=[vtt])
                    else:
                        kb.op(ACT, lambda e, j=j, hf=hf, p=p: e.activation(out=vtt.h[:, j, hf * 512:(hf + 1) * 512], in_=p.ap(), func=AF.Identity),
                              r=[p.t], w=[vtt])
            kb.dma(POOL, VS_d.h[tt * 512:(tt + 1) * 512, :].rearrange("(j p) d -> p j d", p=128), vtt.h[:], r=[vtt], w=[VS_d], owner=vtt)
            p = next_pmm()
            kb.mm([lambda e, k=k, p=p: e.matmul(p.ap(p1=H), lhsT=w_in.h[:, k, 4 * D:4 * D + H], rhs=hT.h[:, k, :],
                                               start=(k == 0), stop=(k == KC - 1)) for k in range(KC)],
                  r=[w_in, hT], w=[p.t])
            eft = ef[tt % 2]
            kb.op(ACT, lambda e, p=p, eft=eft: e.activation(out=eft.h[:], in_=p.ap(p1=H), func=AF.Exp, scale=-1.0, bias=nbf.h[:, 0:1]),
                  r=[p.t, nbf], w=[eft])
            nl = nlt[tt % 2]
            kb.op(ACT, lambda e, eft=eft, nl=nl: e.activation(out=nl.h[:], in_=eft.h[:], func=AF.Ln, bias=cst.h[0:H, 2:3]),
                  r=[eft, cst], w=[nl])
            kb.dma(POOL, NLF_d.h[:, tt * 512:(tt + 1) * 512], nl.h[:], r=[nl], w=[NLF_d], owner=nl)
            if own:
                for oc in range(8):
                    p = next_pmm()
                    kb.mm([lambda e, k=k, oc=oc, p=p: e.matmul(p.ap(), lhsT=w_in.h[:, k, 3 * D + oc * 128:3 * D + (oc + 1) * 128],
                                                              rhs=hT.h[:, k, :], start=(k == 0), stop=(k == KC - 1)) for k in range(KC)],
                          r=[w_in, hT], w=[p.t])
                    i = cnt[0]
                    cnt[0] += 1
                    kqt = kq[i % 3]
                    kb.op(ACT, lambda e, p=p, kqt=kqt: e.activation(out=kqt.h[:], in_=p.ap(), func=AF.Sigmoid), r=[p.t], w=[kqt])
                    if tt == 7:
                        kb.dma(POOL, SG_d.h[oc * 128:(oc + 1) * 128, 0:256], kqt.h[:, 256:512], r=[kqt], w=[SG_d], owner=kqt)
                    else:
                        q0 = 256 + (tt - 8) * 512
                        kb.dma(POOL, SG_d.h[oc * 128:(oc + 1) * 128, q0:q0 + 512], kqt.h[:], r=[kqt], w=[SG_d], owner=kqt)
        kb.stage_end()

    def stage_decay():
        kb.stage_begin()
        nls = [kb.sb(f"nls{i}", [H, 2048], F32) for i in range(2)]
        ones16 = kb.sb("ones16", [H, 2048], F32)
        nf = [kb.sb(f"nf{i}", [H, 2048], F32) for i in range(2)]
        kp = [kb.sb(f"kp{i}", [H, 2048], F32) for i in range(2)]
        r1 = kb.sb("r1", [H, 2048], F32)
        parts = [[kb.sb(f"part{i}_{j}", [H, 2048], BF16) for j in range(3)] for i in range(2)]
        qn = kb.sb("qn", [H, 2048], F32)
        qparts = [[kb.sb(f"qpart{i}_{j}", [H, 2048], BF16) for j in range(3)] for i in range(2)]
        kb.op(POOL, lambda e: e.memset(ones16.h[:], 1.0), w=[ones16])
        for ch in range(4):
            a = nf[ch % 2]
            kpt = kp[ch % 2]
            kb.dma(SP, kpt.h[:], kpad.h[:, ch * 2048:(ch + 1) * 2048], r=[kpad], w=[kpt], owner=kpt)
            nlf = nls[ch % 2]
            kb.dma(SP, nlf.h[:], NLF_d.h[:, ch * 2048:(ch + 1) * 2048], r=[NLF_d], w=[nlf], owner=nlf)
            if ch == 0:
                kb.op(DVE, lambda e, a=a: e.tensor_tensor_scan(out=a.h[:], data0=ones16.h[:], data1=nlf.h[:], initial=0.0,
                                                               op0=ALU.mult, op1=ALU.add), r=[ones16, nlf], w=[a])
            else:
                prev = nf[(ch - 1) % 2]
                kb.op(DVE, lambda e, a=a, prev=prev, ch=ch: e.tensor_tensor_scan(out=a.h[:], data0=ones16.h[:], data1=nlf.h[:],
                                                                                 initial=prev.h[:, 2047:2048], op0=ALU.mult, op1=ALU.add),
                      r=[ones16, nlf, prev], w=[a])
            def split3(src, dst3, n):
                kb.op(DVE, lambda e: e.tensor_copy(out=dst3[0].h[:, 0:n], in_=src.h[:, 0:n]), r=[src], w=[dst3[0]])
                kb.op(DVE, lambda e: e.tensor_tensor(out=r1.h[:, 0:n], in0=src.h[:, 0:n], in1=dst3[0].h[:, 0:n], op=ALU.subtract), r=[src, dst3[0]], w=[r1])
                kb.op(DVE, lambda e: e.tensor_copy(out=dst3[1].h[:, 0:n], in_=r1.h[:, 0:n]), r=[r1], w=[dst3[1]])
                kb.op(DVE, lambda e: e.tensor_tensor(out=r1.h[:, 0:n], in0=r1.h[:, 0:n], in1=dst3[1].h[:, 0:n], op=ALU.subtract), r=[r1, dst3[1]], w=[r1])
                kb.op(DVE, lambda e: e.tensor_copy(out=dst3[2].h[:, 0:n], in_=r1.h[:, 0:n]), r=[r1], w=[dst3[2]])
            if ch >= 1:
                c0 = 1792 if ch == 1 else 0
                n = 2048 - c0
                e0 = 0 if ch == 1 else 256 + (ch - 2) * 2048
                kb.op(DVE, lambda e: e.tensor_scalar(out=qn.h[:, 0:n], in0=a.h[:, c0:2048], scalar1=-1.0, scalar2=None, op0=ALU.mult), r=[a], w=[qn])
                qp = qparts[ch % 2]
                split3(qn, qp, n)
                for j in range(3):
                    kb.dma(POOL, CQ_d.h[:, j, e0:e0 + n], qp[j].h[:, 0:n], r=[qp[j]], w=[CQ_d], owner=qp[j])
            kb.op(DVE, lambda e, a=a, kpt=kpt: e.tensor_tensor(out=kpt.h[:], in0=a.h[:], in1=kpt.h[:], op=ALU.add), r=[a, kpt], w=[kpt])
            pt = parts[ch % 2]
            split3(kpt, pt, 2048)
            for j in range(3):
                kb.dma(POOL, FK_d.h[:, j, ch * 2048:(ch + 1) * 2048], pt[j].h[:], r=[pt[j]], w=[FK_d], owner=pt[j])
        kb.stage_end()

    def stage_attn(pre=None):
        kb.stage_begin()
        nxt_slot = pre() if pre else None
        KA = [kb.sb(f"KA{i}", [70, S], BF16) for i in range(2)]
        QA = [kb.sb(f"QA{i}", [70, EXT], BF16) for i in range(2)]
        VA = [kb.sb(f"VA{i}", [128, 64, 65], BF16) for i in range(2)]
        SGh = [kb.sb(f"SGh{i}", [64, EXT], BF16) for i in range(2)]
        maskf = kb.sb("maskf", [128, 128], F32)
        maskb = kb.sb("maskb", [128, 128], BF16)
        pT = [kb.sb(f"pT{i}", [128, 1024], BF16) for i in range(3)]
        accs = [kb.sb(f"accs{i}", [65, 512], F32) for i in range(2)]
        rden = [kb.sb(f"rden{i}", [65, 512], F32) for i in range(2)]
        rhi = [kb.sb(f"rhi{i}", [65, 512], BF16) for i in range(2)]
        rlo = [kb.sb(f"rlo{i}", [65, 512], BF16) for i in range(2)]
        gs = [kb.sb(f"gs{i}", [64, 512], F32) for i in range(2)]
        ag = [kb.sb(f"ag{i}", [64, 512], BF16) for i in range(2)]
        psS = [PV(0, 1024, "psS0"), PV(1024, 1024, "psS1"), PV(2048, 1024, "psS2")]
        acc = [PV(3072, 512, "acc0")] * 2
        pbc = [PV(3584, 512, "pbc0")] * 2
        kb.dma(SP, maskf.h[:], maskb_d.h[:, :], r=[maskb_d], w=[maskf], owner=maskf)
        kb.op(DVE, lambda e: e.tensor_copy(out=maskb.h[:], in_=maskf.h[:]), r=[maskf], w=[maskb])
        conv = conversion_jobs()
        for i in range(2):
            kb.op(POOL, lambda e, i=i: e.memset(KA[i].h[64:70, :], 1.0), w=[KA[i]])
            kb.op(POOL, lambda e, i=i: e.memset(QA[i].h[64:70, :], 1.0), w=[QA[i]])
            kb.op(POOL, lambda e, i=i: e.memset(VA[i].h[:, :, 64:65], 1.0), w=[VA[i]])
        def load_head(h):
            ka, qa, va, sg = KA[h % 2], QA[h % 2], VA[h % 2], SGh[h % 2]
            kb.dma(SP, ka.h[0:64, :], KT_d.h[h * 64:(h + 1) * 64, :], r=[KT_d], w=[ka], owner=ka)
            kb.dma(SP, ka.h[64:67, :], FK_d.h[h, :, :], r=[FK_d], w=[ka], owner=ka)
            kb.dma(SP, qa.h[0:64, :], QT_d.h[h * 64:(h + 1) * 64, :], r=[QT_d], w=[qa], owner=qa)
            kb.dma(SP, qa.h[67:70, :], CQ_d.h[h, :, :], r=[CQ_d], w=[qa], owner=qa)
            for c4 in range(4):
                kb.dma(SP, va.h[:, c4 * 16:(c4 + 1) * 16, 0:64],
                       VS_d.h[c4 * 2048:(c4 + 1) * 2048, h * 64:(h + 1) * 64].rearrange("(kb p) d -> p kb d", p=128),
                       r=[VS_d], w=[va], owner=va)
            kb.dma(SP, sg.h[:], SG_d.h[h * 64:(h + 1) * 64, :], r=[SG_d], w=[sg], owner=sg)

        groups = []
        for h in range(H):
            for qi, (t0, wd) in enumerate(TILES):
                qb0 = (E0 + t0) // 128
                nkb = qb0 + wd // 128
                for g0 in range(0, nkb, 2):
                    groups.append((h, qi, t0, wd, qb0, nkb, list(range(g0, min(g0 + 2, nkb)))))

        def emit_qk(gidx):
            h, qi, t0, wd, qb0, nkb, grp = groups[gidx]
            ka, qa = KA[h % 2], QA[h % 2]
            ps_ = psS[gidx % 3]
            SW = wd
            fns = []
            for sl, kbi in enumerate(grp):
                di = kbi - qb0
                c0 = 128 * di if di > 0 else 0
                fns.append(lambda e, sl=sl, kbi=kbi, c0=c0, di=di: e.matmul(
                    ps_.ap(c0=sl * SW + c0, c1=sl * SW + wd), lhsT=ka.h[0:70, kbi * 128:(kbi + 1) * 128],
                    rhs=qa.h[0:70, t0 + c0:t0 + wd], start=True, stop=(di < 0), skip_group_check=True))
                if di >= 0:
                    fns.append(lambda e, sl=sl, di=di: e.matmul(
                        ps_.ap(c0=sl * SW + 128 * di, c1=sl * SW + 128 * di + 128), lhsT=identb.h[:], rhs=maskb.h[:],
                        start=False, stop=True, skip_group_check=True))
            kb.mm(fns, r=[ka, qa, identb, maskb], w=[ps_.t])

        def emit_exp_pv(gidx):
            h, qi, t0, wd, qb0, nkb, grp = groups[gidx]
            va, sg = VA[h % 2], SGh[h % 2]
            ps_ = psS[gidx % 3]
            pt_ = pT[gidx % 3]
            ei = h * len(TILES) + qi
            accv = acc[ei % 2]
            SW = wd
            ncol = (len(grp) - 1) * SW + wd
            d0 = grp[0] - qb0
            cs = 128 * d0 if d0 > 0 else 0
            kb.op(ACT, lambda e: e.activation(out=pt_.h[:, cs:ncol], in_=ps_.ap(c0=cs, c1=ncol), func=AF.Exp), r=[ps_.t], w=[pt_])
            fns = []
            for sl, kbi in enumerate(grp):
                di = kbi - qb0
                c0 = 128 * di if di > 0 else 0
                fns.append(lambda e, sl=sl, kbi=kbi, c0=c0: e.matmul(
                    accv.ap(p1=65, c0=c0, c1=wd), lhsT=va.h[:, kbi, 0:65], rhs=pt_.h[:, sl * SW + c0:sl * SW + wd],
                    start=(kbi == 0), stop=(kbi == nkb - 1), skip_group_check=True))
            kb.mm(fns, r=[va, pt_], w=[accv.t])
            if grp[-1] != nkb - 1:
                return
            e2 = ei % 2
            rd, rh, rl, gst, agt, pb = rden[e2], rhi[e2], rlo[e2], gs[e2], ag[e2], pbc[e2]
            asb = accs[e2]
            kb.op(DVE, lambda e: e.tensor_copy(out=asb.h[0:65, 0:wd], in_=accv.ap(p1=65, c1=wd)), r=[accv.t], w=[asb])
            kb.op(DVE, lambda e: e.tensor_scalar(out=rd.h[64:65, 0:wd], in0=asb.h[64:65, 0:wd], scalar1=cst.h[64:65, 3:4],
                                                 scalar2=None, op0=ALU.add), r=[asb, cst], w=[rd])
            kb.op(DVE, lambda e: e.reciprocal(out=rd.h[64:65, 0:wd], in_=rd.h[64:65, 0:wd]), r=[rd], w=[rd])
            kb.op(DVE, lambda e: e.tensor_copy(out=rh.h[64:65, 0:wd], in_=rd.h[64:65, 0:wd]), r=[rd], w=[rh])
            kb.op(DVE, lambda e: e.tensor_tensor(out=rd.h[64:65, 0:wd], in0=rd.h[64:65, 0:wd], in1=rh.h[64:65, 0:wd], op=ALU.subtract),
                  r=[rd, rh], w=[rd])
            kb.op(DVE, lambda e: e.tensor_copy(out=rl.h[64:65, 0:wd], in_=rd.h[64:65, 0:wd]), r=[rd], w=[rl])
            pending.append((gidx + 7, lambda: epilogue_b(h, t0, wd, asb, rh, rl, gst, agt, pb, sg)))

        def epilogue_b(h, t0, wd, asb, rh, rl, gst, agt, pb, sg):
            kb.mm([lambda e: e.matmul(pb.ap(p1=64, c1=wd), lhsT=onesb.h[64:65, 0:64], rhs=rh.h[64:65, 0:wd], start=True, stop=False),
                   lambda e: e.matmul(pb.ap(p1=64, c1=wd), lhsT=onesb.h[64:65, 0:64], rhs=rl.h[64:65, 0:wd], start=False, stop=True)],
                  r=[onesb, rh, rl], w=[pb.t])
            kb.op(DVE, lambda e: e.tensor_tensor(out=gst.h[:, 0:wd], in0=pb.ap(p1=64, c1=wd), in1=sg.h[:, t0:t0 + wd], op=ALU.mult),
                  r=[pb.t, sg], w=[gst])
            kb.op(DVE, lambda e: e.tensor_tensor(out=agt.h[:, 0:wd], in0=asb.h[0:64, 0:wd], in1=gst.h[:, 0:wd], op=ALU.mult),
                  r=[asb, gst], w=[agt])
            kb.dma(POOL, AT_d.h[h * 64:(h + 1) * 64, t0:t0 + wd], agt.h[:, 0:wd], r=[agt], w=[AT_d], owner=agt)

        pending = []
        load_head(0)
        loaded = 0
        LAG = 2
        for gidx in range(len(groups) + LAG):
            if gidx < len(groups):
                emit_qk(gidx)
            if conv and gidx >= 30 and gidx % 40 == 0:
                conv.pop(0)()
            while pending and pending[0][0] <= gidx:
                pending.pop(0)[1]()
            if gidx >= LAG:
                emit_exp_pv(gidx - LAG)
                hcur = groups[gidx - LAG][0]
                if hcur == loaded and hcur + 1 < H and not pending and (gidx < 12 or groups[gidx - 12][0] == hcur):
                    load_head(hcur + 1)
                    loaded = hcur + 1
        while pending:
            pending.pop(0)[1]()
        while conv:
            conv.pop(0)()
        kb.stage_end()
        return nxt_slot

    def stage_out(tag, inT_d, kcin, W, gidx, x_in_ap, x_in_t, x_out, nxt, final=False, pre=None, in_bufs=2):
        kb.stage_begin()
        nxt_slot = pre() if pre else None
        ctx = NormCtx(tag, need_xn=not final, nbuf=1)
        ins = [kb.sb(f"in{tag}{i}", [128, kcin, 512], BF16) for i in range(in_bufs)] * (2 // in_bufs)
        xts = [kb.sb(f"xo{tag}{i}", [128, 4, D], F32) for i in range(2)]
        tmp = [kb.sb(f"tmp{tag}{i}", [128, 512], F32) for i in range(2)]
        gb = kb.sb("gb" + tag, [128, D], F32)
        kb.dma(SP, gb.h[:], GB_d.h[gidx], r=[GB_d], w=[gb], owner=gb)
        py = [PV(1024 + 512 * i, 512, f"py{tag}{i}") for i in range(4)]
        if final:
            fg = kb.sb("fg", [128, D], F32)
            kb.dma(SP, fg.h[:], final_g.h[:, :], r=[final_g], w=[fg], owner=fg)
        else:
            hTs = [kb.sb(f"hTo{tag}", [128, KC, 512], BF16)] * 2
        itc = [0]

        def part1(ti):
            t0, wd = TILES[ti]
            nb = wd // 128
            it_ = ins[ti % 2]
            xt = xts[ti % 2]
            kb.dma(SP, it_.h[:, :, 0:wd], inT_d.h[:, t0:t0 + wd].rearrange("(k p) t -> p k t", p=128), r=[inT_d], w=[it_], owner=it_)
            kb.dma(SP, xt.h[:, 0:nb, :], x_in_ap[t0:t0 + wd, :].rearrange("(j p) d -> p j d", p=128), r=[x_in_t], w=[xt], owner=xt)
            for j in range(nb):
                for hf in range(2):
                    p = py[itc[0] % 4]
                    tm = tmp[itc[0] % 2]
                    itc[0] += 1
                    kb.mm([lambda e, k=k, j=j, hf=hf, p=p: e.matmul(p.ap(), lhsT=it_.h[:, k, j * 128:(j + 1) * 128],
                                                                   rhs=W.h[:, k, hf * 512:(hf + 1) * 512], start=(k == 0), stop=(k == kcin - 1))
                           for k in range(kcin)], r=[it_, W], w=[p.t])
                    kb.op(DVE, lambda e, p=p, tm=tm, hf=hf: e.tensor_tensor(out=tm.h[:], in0=p.ap(), in1=gb.h[:, hf * 512:(hf + 1) * 512], op=ALU.mult),
                          r=[p.t, gb], w=[tm])
                    kb.op(POOL, lambda e, tm=tm, j=j, hf=hf: e.tensor_tensor(out=xt.h[:, j, hf * 512:(hf + 1) * 512], in0=xt.h[:, j, hf * 512:(hf + 1) * 512],
                                                                            in1=tm.h[:], op=ALU.add), r=[tm, xt], w=[xt])
            if not final and x_out is not None:
                kb.dma(POOL, x_out.h[t0:t0 + wd, :].rearrange("(j p) d -> p j d", p=128), xt.h[:, 0:nb, :], r=[xt], w=[x_out], owner=xt)

        def part2(ti):
            t0, wd = TILES[ti]
            nb = wd // 128
            xt = xts[ti % 2]
            if final:
                if t0 + wd > HALO:
                    rs = rstd_of(ctx, xt, nb)
                    ctx.it += 1
                    o = xt
                    for j in range(nb):
                        kb.op(DVE, lambda e, j=j, o=o, rs=rs: e.scalar_tensor_tensor(out=o.h[:, j, :], in0=xt.h[:, j, :], scalar=rs.h[:, j:j + 1], in1=fg.h[:],
                                                                                     op0=ALU.mult, op1=ALU.mult), r=[xt, rs, fg], w=[o])
                    kb.dma(POOL, out_d.h[t0 - HALO:t0 - HALO + wd, :].rearrange("(j p) d -> p j d", p=128), o.h[:, 0:nb, :], r=[o], w=[out_d], owner=o)
            else:
                hT = hTs[ti % 2]
                l, w_ = nxt
                norm_to_hT(ctx, xt, nb, l, w_, hT)
                kb.dma(POOL, HT_d.h[:, t0:t0 + wd].rearrange("(k p) t -> p k t", p=128), hT.h[:, :, 0:wd], r=[hT], w=[HT_d], owner=hT)

        part1(0)
        if nxt_slot is not None and nxt_slot[2] is not None:
            nxt_slot[2]()
        for ti in range(len(TILES)):
            if ti + 1 < len(TILES):
                part1(ti + 1)
            part2(ti)
        kb.stage_end()
        return nxt_slot

    def stage_ffn_up(l, W, pre=None):
        kb.stage_begin()
        nxt_slot = pre() if pre else None
        cw = kb.sb("cw", [128, 4, NFC], F32)
        kb.dma(SP, cw.h[:], ffn_cw.h[l], r=[ffn_cw], w=[cw], owner=cw)
        tm = kb.sb("tmk", [128, HALO], F32)
        kb.dma(SP, tm.h[:], tmask.h[:, :], r=[tmask], w=[tm], owner=tm)
        hx = [kb.sb(f"hx{i}", [128, KC, 512], BF16) for i in range(2)]
        carry = [kb.sb(f"cc{i}", [128, NFC, 2], F32) for i in range(2)]
        Cg = [kb.sb(f"Cg{i}", [128, 512], F32) for i in range(3)]
        Cv = [kb.sb(f"Cv{i}", [128, 512], F32) for i in range(2)] + [None]
        ut = [kb.sb(f"ut{i}", [128, 22, 512], BF16) for i in range(2)]
        pa = [PV(512 * i, 512, f"pa{i}") for i in range(8)]
        kb.op(POOL, lambda e: e.memset(carry[0].h[:], 0.0), w=[carry[0]])
        fix = [kb.sb(f"fix{i}", [128, NFC, 2], F32) for i in range(2)]
        ftmp = kb.sb("ftmp", [128, NFC], F32)
        pi = 0
        ci = 0
        for ti, (t0, wd) in enumerate(TILES):
            hxt = hx[ti % 2]
            u = ut[ti % 2]
            cold, cnew = carry[ti % 2], carry[(ti + 1) % 2]
            kb.dma(SP, hxt.h[:, :, 0:wd], HT_d.h[:, t0:t0 + wd].rearrange("(k p) t -> p k t", p=128), r=[HT_d], w=[hxt], owner=hxt)
            if ti == 0:
                if nxt_slot is not None and nxt_slot[2] is not None:
                    nxt_slot[2]()
                for k in range(KC):
                    kb.op(DVE, lambda e, k=k: e.tensor_tensor(out=hxt.h[:, k, 0:HALO], in0=hxt.h[:, k, 0:HALO], in1=tm.h[:], op=ALU.mult),
                          r=[hxt, tm], w=[hxt])
            fx = fix[ti % 2]
            kb.op(POOL, lambda e: e.tensor_tensor(out=fx.h[:, :, 0], in0=cold.h[:, :, 0], in1=cw.h[:, 0, :], op=ALU.mult), r=[cold, cw], w=[fx])
            kb.op(POOL, lambda e: e.tensor_tensor(out=ftmp.h[:], in0=cold.h[:, :, 1], in1=cw.h[:, 1, :], op=ALU.mult), r=[cold, cw], w=[ftmp])
            kb.op(POOL, lambda e: e.tensor_tensor(out=fx.h[:, :, 0], in0=fx.h[:, :, 0], in1=ftmp.h[:], op=ALU.add), r=[fx, ftmp], w=[fx])
            kb.op(POOL, lambda e: e.tensor_tensor(out=fx.h[:, :, 1], in0=cold.h[:, :, 1], in1=cw.h[:, 0, :], op=ALU.mult), r=[cold, cw], w=[fx])
            for fp in range(22):
                cs = []
                for wi, fc in enumerate((fp, 22 + fp)):
                    p = pa[pi % 8]
                    pi += 1
                    kb.mm([lambda e, k=k, fc=fc, p=p: e.matmul(p.ap(c1=wd), lhsT=W.h[:, k, fc * 128:(fc + 1) * 128], rhs=hxt.h[:, k, 0:wd],
                                                              start=(k == 0), stop=(k == KC - 1)) for k in range(KC)], r=[W, hxt], w=[p.t])
                    C = Cg[ci % 3] if wi == 0 else Cv[ci % 2]
                    kb.op(ACT, lambda e, C=C, p=p, fc=fc: e.activation(out=C.h[:, 0:wd], in_=p.ap(c1=wd), func=AF.Identity,
                                                                       scale=cw.h[:, 2, fc:fc + 1], bias=cw.h[:, 3, fc:fc + 1]), r=[p.t, cw], w=[C])
                    kb.op(ACT, lambda e, p=p, fc=fc: e.activation(out=cnew.h[:, fc, :], in_=p.ap(c0=wd - 2, c1=wd), func=AF.Identity),
                          r=[p.t], w=[cnew])
                    kb.op(DVE, lambda e, C=C, p=p, fc=fc: e.scalar_tensor_tensor(out=C.h[:, 1:wd], in0=p.ap(c0=0, c1=wd - 1), scalar=cw.h[:, 1, fc:fc + 1],
                                                                                 in1=C.h[:, 1:wd], op0=ALU.mult, op1=ALU.add), r=[p.t, cw, C], w=[C])
                    kb.op(DVE, lambda e, C=C, p=p, fc=fc: e.scalar_tensor_tensor(out=C.h[:, 2:wd], in0=p.ap(c0=0, c1=wd - 2), scalar=cw.h[:, 0, fc:fc + 1],
                                                                                 in1=C.h[:, 2:wd], op0=ALU.mult, op1=ALU.add), r=[p.t, cw, C], w=[C])
                    kb.op(POOL, lambda e, C=C, fc=fc: e.tensor_tensor(out=C.h[:, 0:2], in0=C.h[:, 0:2], in1=fx.h[:, fc, :], op=ALU.add), r=[fx, C], w=[C])
                    cs.append(C)
                G = cs[0]
                ci += 1
                kb.op(ACT, lambda e, G=G: e.activation(out=G.h[:, 0:wd], in_=G.h[:, 0:wd], func=AF.Silu), r=[G], w=[G])
                kb.op(POOL, lambda e, G=G, c=cs[1], fp=fp: e.tensor_tensor(out=u.h[:, fp, 0:wd], in0=G.h[:, 0:wd], in1=c.h[:, 0:wd], op=ALU.mult),
                      r=[G, cs[1]], w=[u])
            kb.dma(POOL, UT_d.h[:, t0:t0 + wd].rearrange("(k p) t -> p k t", p=128), u.h[:, :, 0:wd], r=[u], w=[UT_d], owner=u)
        kb.stage_end()
        return nxt_slot

    def stage_sgu(W, pre=None):
        kb.stage_begin()
        nxt_slot = pre() if pre else None
        ws = kb.sb("ws", [128, 8, 128], BF16)
        buT = kb.sb("buT", [128, 16], F32)
        bvb = kb.sb("bvb", [1, 2 * D], BF16)
        vg = kb.sb("vg", [128, 2 * D], F32)
        vb = kb.sb("vb", [128, 2 * D], F32)
        bs2 = kb.sb("bs2", [2, 1024], F32)
        bsh = kb.sb("bsh", [2, 1024], BF16)
        bsr = kb.sb("bsr", [2, 1024], F32)
        bsl = kb.sb("bsl", [2, 1024], BF16)
        bshl = kb.sb("bshl", [2, 1024], BF16)
        s2 = kb.sb("s2", [2, 2], F32)
        hTs = [kb.sb(f"hs{i}", [128, KC, 512], BF16) for i in range(2)]
        uTs = [kb.sb(f"uT{i}", [128, 16, 512], BF16) for i in range(2)]
        gvs = [kb.sb(f"gv{i}", [128, 2 * D], F32) for i in range(3)]
        vlns = [kb.sb(f"vln{i}", [128, 2 * D], BF16) for i in range(2)]
        st6s = [kb.sb(f"st6{i}", [128, 24], F32) for i in range(3)]
        mvs = [kb.sb(f"mv{i}", [128, 2], F32) for i in range(3)]
        rstds = [kb.sb(f"rstdv{i}", [128, 2], F32) for i in range(3)]
        bi = 0
        yT = [kb.sb("yT0", [128, 16, 512], BF16)] * 2
        pu = [PV(0, 512, "pu0"), PV(512, 512, "pu1")]
        pv = PV(1024, 2048, "pv")
        pm = [PV(3072, 512, "pm0"), PV(3584, 512, "pm1")]
        kb.dma(POOL, ws.h[:], sgu_wsT.h[:, :, :], r=[sgu_wsT], w=[ws], owner=ws)
        kb.op(POOL, lambda e: e.memset(ws.h[64:128, :, 0:64], 0.0), w=[ws])
        kb.dma(SP, buT.h[:], sgu_buT.h[:, :], r=[sgu_buT], w=[buT], owner=buT)
        kb.dma(POOL, bvb.h[:], sgu_bv.h[:, :], r=[sgu_bv], w=[bvb], owner=bvb)
        kb.dma(SP, vg.h[:], sgu_vg.h[:, :], r=[sgu_vg], w=[vg], owner=vg)
        kb.dma(SP, vb.h[:], sgu_vb.h[:, :], r=[sgu_vb], w=[vb], owner=vb)
        kb.dma(SP, bs2.h[:], sgu_bs2.h[:, :], r=[sgu_bs2], w=[bs2], owner=bs2)
        kb.dma(SP, s2.h[:], sel2.h[:, :], r=[sel2], w=[s2], owner=s2)
        kb.op(DVE, lambda e: e.tensor_copy(out=bsh.h[:], in_=bs2.h[:]), r=[bs2], w=[bsh])
        kb.op(DVE, lambda e: e.tensor_tensor(out=bsr.h[:], in0=bs2.h[:], in1=bsh.h[:], op=ALU.subtract), r=[bs2, bsh], w=[bsr])
        kb.op(DVE, lambda e: e.tensor_copy(out=bsl.h[:], in_=bsr.h[:]), r=[bsr], w=[bsl])
        kb.op(DVE, lambda e: e.tensor_scalar(out=bsr.h[:], in0=bsl.h[:], scalar1=s2.h[:, 1:2], scalar2=None, op0=ALU.mult), r=[bsl, s2], w=[bsr])
        kb.op(DVE, lambda e: e.scalar_tensor_tensor(out=bshl.h[:], in0=bsh.h[:], scalar=s2.h[:, 0:1], in1=bsr.h[:], op0=ALU.mult, op1=ALU.add),
              r=[bsh, s2, bsr], w=[bshl])
        blocks = [(ti, j) for ti, (t0, wd) in enumerate(TILES) for j in range(wd // 128)]
        pui = [0]
        pmi = [0]

        def u_chunks(ti, fcs):
            t0, wd = TILES[ti]
            hT = hTs[ti % 2]
            for fc in fcs:
                p = pu[pui[0] % 2]
                pui[0] += 1
                kb.mm([lambda e, k=k, fc=fc, p=p: e.matmul(p.ap(c1=wd), lhsT=W.h[:, k, fc * 128:(fc + 1) * 128], rhs=hT.h[:, k, 0:wd],
                                                          start=(k == 0), stop=(k == KC - 1)) for k in range(KC)], r=[W, hT], w=[p.t])
                kb.op(ACT, lambda e, fc=fc, p=p: e.activation(out=uTs[ti % 2].h[:, fc, 0:wd], in_=p.ap(c1=wd), func=AF.Gelu_apprx_tanh,
                                                              bias=buT.h[:, fc:fc + 1]), r=[p.t, buT], w=[uTs[ti % 2]])

        def load_tile(ti):
            t0, wd = TILES[ti]
            hT = hTs[ti % 2]
            kb.dma(SP, hT.h[:, :, 0:wd], HT_d.h[:, t0:t0 + wd].rearrange("(k p) t -> p k t", p=128), r=[HT_d], w=[hT], owner=hT)

        def pe_v(bidx):
            ti, j = blocks[bidx]
            hT = hTs[ti % 2]
            fns = []
            for q4 in range(4):
                fns += [lambda e, k=k, q4=q4: e.matmul(pv.ap(c0=q4 * 512, c1=(q4 + 1) * 512), lhsT=hT.h[:, k, j * 128:(j + 1) * 128],
                                                      rhs=W.h[:, k, 2 * D + q4 * 512:2 * D + (q4 + 1) * 512], start=(k == 0), stop=False)
                        for k in range(KC)]
                fns.append(lambda e, q4=q4: e.matmul(pv.ap(c0=q4 * 512, c1=(q4 + 1) * 512), lhsT=onesb.h[0:1, 0:128],
                                                     rhs=bvb.h[0:1, q4 * 512:(q4 + 1) * 512], start=False, stop=True))
            kb.mm(fns, r=[W, hT, onesb, bvb], w=[pv.t])

        def act_gelu(bidx):
            gv = gvs[bidx % 3]
            kb.op(ACT, lambda e: e.activation(out=gv.h[:], in_=pv.ap(), func=AF.Gelu_apprx_tanh), r=[pv.t], w=[gv])

        def dve_stats(bidx):
            gv, st6, mv = gvs[bidx % 3], st6s[bidx % 3], mvs[bidx % 3]
            for c in range(4):
                kb.op(DVE, lambda e, c=c: e.bn_stats(out=st6.h[:, c * 6:(c + 1) * 6], in_=gv.h[:, c * 512:(c + 1) * 512]), r=[gv], w=[st6])
            kb.op(DVE, lambda e: e.bn_aggr(out=mv.h[:], in_=st6.h[:]), r=[st6], w=[mv])

        def norm_chain(bidx):
            gv, mv, rstd = gvs[bidx % 3], mvs[bidx % 3], rstds[bidx % 3]
            kb.op(ACT, lambda e: e.activation(out=rstd.h[:, 0:1], in_=mv.h[:, 1:2], func=AF.Ln, bias=cst.h[:, 0:1]), r=[mv, cst], w=[rstd])
            kb.op(ACT, lambda e: e.activation(out=rstd.h[:, 0:1], in_=rstd.h[:, 0:1], func=AF.Exp, scale=-0.5), r=[rstd], w=[rstd])
            kb.op(DVE, lambda e: e.scalar_tensor_tensor(out=rstd.h[:, 1:2], in0=mv.h[:, 0:1], scalar=-1.0, in1=rstd.h[:, 0:1],
                                                        op0=ALU.mult, op1=ALU.mult), r=[mv, rstd], w=[rstd])
            kb.op(ACT, lambda e: e.activation(out=gv.h[:], in_=gv.h[:], func=AF.Identity, scale=rstd.h[:, 0:1], bias=rstd.h[:, 1:2]),
                  r=[gv, rstd], w=[gv])

        def scale_bias(bidx):
            gv, vln = gvs[bidx % 3], vlns[bidx % 2]
            kb.op(DVE, lambda e: e.tensor_tensor(out=gv.h[:], in0=gv.h[:], in1=vg.h[:], op=ALU.mult), r=[gv, vg], w=[gv])
            kb.op(POOL, lambda e: e.tensor_tensor(out=vln.h[:], in0=gv.h[:], in1=vb.h[:], op=ALU.add), r=[gv, vb], w=[vln])

        def mix_y(bidx):
            ti, j = blocks[bidx]
            t0, wd = TILES[ti]
            nb = wd // 128
            y = yT[ti % 2]
            uT = uTs[ti % 2]
            vln = vlns[bidx % 2]
            for c4 in range(4):
                p = pm[pmi[0] % 2]
                pmi[0] += 1
                fns = []
                for cc in range(4):
                    fc = c4 * 4 + cc
                    g = fc // 2
                    fns.append(lambda e, fc=fc, g=g, cc=cc, p=p: e.matmul(p.ap(c0=cc * 128, c1=(cc + 1) * 128), lhsT=vln.h[:, fc * 128:(fc + 1) * 128],
                                                                         rhs=ws.h[:, g, :], start=True, stop=False, skip_group_check=True))
                    fns.append(lambda e, g=g, cc=cc, p=p: e.matmul(p.ap(c0=cc * 128, c1=(cc + 1) * 128), lhsT=onesb.h[0:2, 0:128],
                                                                  rhs=bshl.h[0:2, g * 128:(g + 1) * 128], start=False, stop=True, skip_group_check=True))
                kb.mm(fns, r=[vln, ws, onesb, bshl], w=[p.t])
                kb.op(DVE, lambda e, c4=c4, p=p: e.tensor_tensor(
                    out=y.h[:, c4 * 4:(c4 + 1) * 4, j * 128:(j + 1) * 128],
                    in0=p.ap().rearrange("p (c t) -> p c t", c=4),
                    in1=uT.h[:, c4 * 4:(c4 + 1) * 4, j * 128:(j + 1) * 128], op=ALU.mult), r=[p.t, uT], w=[y])
            if j == nb - 1:
                kb.dma(POOL, YT_d.h[:, t0:t0 + wd].rearrange("(k p) t -> p k t", p=128), y.h[:, :, 0:wd], r=[y], w=[YT_d], owner=y)

        load_tile(0)
        u_chunks(0, range(16))
        load_tile(1)
        NB = len(blocks)
        pe_v(0)
        act_gelu(0)
        dve_stats(0)
        pe_v(1)
        act_gelu(1)
        dve_stats(1)
        for bidx in range(NB):
            ti, j = blocks[bidx]
            nb = TILES[ti][1] // 128
            if bidx + 2 < NB:
                pe_v(bidx + 2)
            norm_chain(bidx)
            if bidx + 2 < NB:
                act_gelu(bidx + 2)
            scale_bias(bidx)
            if bidx + 2 < NB:
                dve_stats(bidx + 2)
            if ti + 1 < len(TILES):
                per = 16 // nb
                u_chunks(ti + 1, range(j * per, (j + 1) * per))
            mix_y(bidx)
            if j == nb - 1 and ti + 2 < len(TILES):
                load_tile(ti + 2)
        kb.stage_end()
        return nxt_slot

    def dump(name, t, shape):
        if name in kb.taps:
            d = kb.dram("dbg_" + name, shape, F32, kind="ExternalOutput")
            kb.dma(SP, d.h, t.h[:], r=[t], w=[d], owner=t)

    def run_all():
        w_in = prefetch_w("in", KC, 4 * D + H, "left", cast_src=fox_w_in.h)
        stage_adaln()
        if upto < 1:
            return
        stage_proj0(w_in[0])
        free_w(w_in)
        stage_decay()
        if upto < 2:
            return
        s_out = stage_attn(pre=lambda: prefetch_w("out", 8, D, "right", cast_src=fox_w_out.h))
        if upto < 3:
            return
        s_up0 = stage_out("a", AT_d, 8, s_out[0], 0, xv.h[E0:S, :], xv, XA_d, (0, 1), pre=lambda: prefetch_w("up0", KC, 2 * DFF, "left", defer=True))
        free_w(s_out)
        s_dn0 = stage_ffn_up(0, s_up0[0], pre=lambda: prefetch_w("dn0", 22, D, "right", defer=True))
        free_w(s_up0)
        stage_out("b", UT_d, 22, s_dn0[0], 1, XA_d.h, XA_d, XB_d, (1, 0))
        free_w(s_dn0)
        s_sgi = prefetch_w("sgi", KC, 4 * D, "left")
        stage_sgu(s_sgi[0])
        free_w(s_sgi)
        s_sgo = prefetch_w("sgo", 16, D, "right")
        s_up1 = stage_out("c", YT_d, 16, s_sgo[0], 2, XB_d.h, XB_d, XA_d, (1, 1), pre=lambda: prefetch_w("up1", KC, 2 * DFF, "left", defer=True), in_bufs=1)
        free_w(s_sgo)
        s_dn1 = stage_ffn_up(1, s_up1[0], pre=lambda: prefetch_w("dn1", 22, D, "right", defer=True))
        free_w(s_up1)
        stage_out("d", UT_d, 22, s_dn1[0], 3, XA_d.h, XA_d, None, None, final=True)
        free_w(s_dn1)

    run_all()
    kb.barrier()
    return kb


def make_in_maps(inp):
    f = lambda a: np.ascontiguousarray(a, dtype=np.float32)
    x = inp["x"]
    maps = []
    maskb = np.where(np.arange(128)[:, None] > np.arange(128)[None, :], NEG, 0.0).astype(np.float32)
    ngT = np.stack([inp["norm1_g"][0], inp["norm2_g"][0], inp["norm1_g"][1], inp["norm2_g"][1]])
    ngT = f(ngT.reshape(4, KC, 128).transpose(2, 0, 1).reshape(128, 4 * KC))
    ada_b = inp["ada_b"]
    ada_bT = f(ada_b.reshape(2, 48, 128).transpose(2, 0, 1).reshape(128, 96))
    bg = np.concatenate([ada_b[l, (2 + 3 * w) * D:(3 + 3 * w) * D] for l in range(2) for w in range(2)])
    ada_bg = f(np.broadcast_to(bg[None, :], (128, 4 * D)))
    cw = np.concatenate([inp["ffn_conv_w"], inp["ffn_conv_b"][:, None, :]], axis=1)
    ffn_cw = f(cw.reshape(2, 4, NFC, 128).transpose(0, 3, 1, 2))
    shared = {
        "maskb": maskb,
        "ada_w": f(inp["ada_w"]), "ada_bT": ada_bT, "ada_bg": ada_bg, "ngT": ngT,
        "final_g": f(np.broadcast_to(inp["final_g"][None, :], (128, D))),
        "fox_w_in": f(inp["fox_w_in"][0]), "fox_bf": f(inp["fox_b_f"][0].reshape(H, 1)),
        "fox_qg": f(np.tile(inp["fox_q_gain"][0], 2).reshape(128, 1)),
        "fox_kg": f(np.tile(inp["fox_k_gain"][0], 2).reshape(128, 1)),
        "fox_w_out": f(inp["fox_w_out"][0]),
        "sgu_w_in": f(inp["sgu_w_in"][0]),
        "sgu_buT": f(inp["sgu_b_in"][0][:2 * D].reshape(16, 128).T),
        "sgu_bv": f(inp["sgu_b_in"][0][2 * D:].reshape(1, 2 * D)),
        "sgu_vg": f(np.broadcast_to(inp["sgu_v_gain"][0][None, :], (128, 2 * D))),
        "sgu_vb": f(np.broadcast_to(inp["sgu_v_bias"][0][None, :], (128, 2 * D))),
        "sgu_wsT": f(inp["sgu_w_s"][0].transpose(2, 0, 1)),
        "sgu_bs2": f(np.broadcast_to(inp["sgu_b_s"][0].reshape(1, 8 * 128), (2, 1024))),
        "sel2": np.eye(2, dtype=np.float32),
        "sgu_w_out": f(inp["sgu_w_out"][0]),
        "ffn_w_up": f(inp["ffn_w_up"]), "ffn_cw": ffn_cw, "ffn_w_down": f(inp["ffn_w_down"]),
    }
    for c in range(8):
        b, half = c // 2, c % 2
        if half == 1:
            xvv = f(x[b])
            kp = np.zeros((H, S), np.float32)
            tm = np.ones((128, HALO), np.float32)
        else:
            xvv = np.zeros((S, D), np.float32)
            xvv[OWN:] = x[b, :OWN]
            kp = np.zeros((H, S), np.float32)
            kp[:, :OWN] = NEG
            tm = np.zeros((128, HALO), np.float32)
        m = dict(shared)
        m.update({"xv": xvv, "cvec": f(inp["c"][b].reshape(KC, 128).T), "kpad": kp, "tmask": tm})
        maps.append(m)
    return maps


def kernel(**inputs):
    inp = {k: np.asarray(v) for k, v in inputs.items()}
    kb = build()
    maps = make_in_maps(inp)
    res = run_bass_kernel_spmd(kb.nc, maps, core_ids=list(range(8)))
    out = np.zeros((4, S, D), np.float32)
    for c in range(8):
        b, half = c // 2, c % 2
        out[b, half * OWN:(half + 1) * OWN] = res.results[c]["out"]
    return out
```

```python
import numpy as np
from contextlib import ExitStack
import concourse.bass as bass
import concourse.mybir as mybir
from concourse.bass_utils import run_bass_kernel_spmd

F32 = mybir.dt.float32
BF16 = mybir.dt.bfloat16
AF = mybir.ActivationFunctionType
ALU = mybir.AluOpType

D = 1024
KC = 8
S = 8192
OWN = 4096
HALO = 256
EXT = OWN + HALO
E0 = S - EXT
H = 16
DH = 64
DFF = 2816
NFC = 44
NEG = -30000.0
STRICT = False
EPS = 1e-6
TILES = [(0, 256)] + [(256 + 512 * i, 512) for i in range(8)]


class Eng:
    def __init__(self, kb, h, name):
        self.h = h
        self.name = name
        self.sem = kb.new_sem("e_" + name)
        self.cnt = 0
        self.waited = {}

    def wait(self, deps):
        for (sname, sem), val in deps.items():
            if self.waited.get(sname, 0) < val:
                self.h.wait_ge(sem, val)
                self.waited[sname] = val


class T:
    def __init__(self, kb, h, name):
        self.kb = kb
        self.h = h
        self.name = name
        self.w = {}
        self.r = {}
        self.dsem = {}

    def __getitem__(self, k):
        return self.h[k]


def _merge(dst, src, skip=None):
    for k, v in src.items():
        if skip is not None and k[0] == skip:
            continue
        if dst.get(k, 0) < v:
            dst[k] = v


class KB:
    def __init__(self, taps=()):
        self.nc = bass.Bass("TRN2", target_bir_lowering=False)
        self.taps = set(taps)
        self.nsem = 0
        self.all_tiles = []
        nc = self.nc
        self.PE = Eng(self, nc.tensor, "pe")
        self.ACT = Eng(self, nc.scalar, "act")
        self.DVE = Eng(self, nc.vector, "dve")
        self.POOL = Eng(self, nc.gpsimd, "pool")
        self.SP = Eng(self, nc.sync, "sp")
        self.engs = [self.PE, self.ACT, self.DVE, self.POOL, self.SP]
        self.stack = None
        self.uid = 0
        self.dsems = []
        self.dpool = {"sp": [], "pool": [], "act": []}
        self.stage_owners = []
        self.marks = []

    def new_sem(self, name):
        self.nsem += 1
        return (name, self.nc.semaphore(name).__enter__())

    def dram(self, name, shape, dt, kind="Internal"):
        if kind == "Internal" and name in self.taps:
            kind = "ExternalOutput"
        t = T(self, self.nc.dram_tensor(name, list(shape), dt, kind=kind).ap(), name)
        return t

    def sb(self, name, shape, dt, persist=False):
        self.uid += 1
        g = self.nc.sbuf_tensor(f"{name}_{self.uid}", list(shape), dt)
        h = g.__enter__() if persist or self.stack is None else self.stack.enter_context(g)
        t = T(self, h, name)
        self.all_tiles.append(t)
        return t

    def ps(self, name, shape, dt):
        g = self.nc.psum_tensor(name, list(shape), dt)
        t = T(self, g.__enter__(), name)
        self.all_tiles.append(t)
        return t

    def _deps(self, eng, r, w):
        deps = {}
        for t in r:
            _merge(deps, t.w)
        skip = eng.sem[0] if (not STRICT or eng.name == "pe") else None
        for t in w:
            _merge(deps, t.w, skip=skip)
            _merge(deps, t.r, skip=skip)
        return deps

    def _commit(self, tok, r, w):
        for t in w:
            t.w = {tok[0]: tok[1]}
            t.r = {}
        for t in r:
            if t in w:
                continue
            if t.r.get(tok[0], 0) < tok[1]:
                t.r[tok[0]] = tok[1]

    def op(self, eng, fn, r=(), w=()):
        eng.wait(self._deps(eng, r, w))
        ins = fn(eng.h)
        eng.cnt += 1
        ins.then_inc(eng.sem[1], 1)
        self._commit((eng.sem, eng.cnt), r, w)

    def mm(self, fns, r=(), w=()):
        eng = self.PE
        eng.wait(self._deps(eng, r, w))
        ins = None
        for fn in fns:
            ins = fn(eng.h)
        eng.cnt += 1
        ins.then_inc(eng.sem[1], 1)
        self._commit((eng.sem, eng.cnt), r, w)

    def dma(self, q, out_ap, in_ap, r=(), w=(), owner=None):
        if q.name not in owner.dsem:
            if self.dpool[q.name]:
                ds = self.dpool[q.name].pop()
            else:
                ds = [self.new_sem("dma" + str(self.nsem)), 0]
                self.dsems.append(ds)
            owner.dsem[q.name] = ds
            if self.stack is not None:
                self.stage_owners.append((owner, q.name))
        deps = {}
        for t in r:
            _merge(deps, t.w)
        for t in w:
            _merge(deps, t.w)
            _merge(deps, t.r)
        q.wait(deps)
        ins = q.h.dma_start(out=out_ap, in_=in_ap)
        ds = owner.dsem[q.name]
        ds[1] += 1
        ins.then_inc(ds[0][1], 16)
        self._commit((ds[0], 16 * ds[1]), r, w)

    def barrier(self):
        deps = {}
        for e in self.engs:
            if e.cnt:
                deps[e.sem] = e.cnt
        for ds in self.dsems:
            if ds[1]:
                deps[ds[0]] = 16 * ds[1]
        for e in self.engs:
            e.wait(deps)

    def stage_begin(self):
        self.stack = ExitStack()

    def stage_end(self):
        self.barrier()
        for t, qn in self.stage_owners:
            self.dpool[qn].append(t.dsem.pop(qn))
        self.stage_owners = []
        self.stack.close()
        self.stack = None
        self.marks.append(self.PE.cnt)


def build(taps=(), upto=99):
    kb = KB(taps)
    nc = kb.nc
    PE, ACT, DVE, POOL, SP = kb.PE, kb.ACT, kb.DVE, kb.POOL, kb.SP

    def din(name, shape, dt=F32):
        return kb.dram(name, shape, dt, kind="ExternalInput")

    xv = din("xv", [S, D])
    cvec = din("cvec", [128, KC])
    kpad = din("kpad", [H, S])
    tmask = din("tmask", [128, HALO])
    maskb_d = din("maskb", [128, 128])
    ada_w = din("ada_w", [2, D, 6 * D])
    ada_bT = din("ada_bT", [128, 96])
    ada_bg = din("ada_bg", [128, 4 * D])
    ngT = din("ngT", [128, 4 * KC])
    final_g = din("final_g", [128, D])
    fox_w_in = din("fox_w_in", [D, 4 * D + H])
    fox_bf = din("fox_bf", [H, 1])
    fox_qg = din("fox_qg", [128, 1])
    fox_kg = din("fox_kg", [128, 1])
    fox_w_out = din("fox_w_out", [D, D])
    sgu_w_in = din("sgu_w_in", [D, 4 * D])
    sgu_buT = din("sgu_buT", [128, 16])
    sgu_bv = din("sgu_bv", [1, 2 * D])
    sgu_vg = din("sgu_vg", [128, 2 * D])
    sgu_vb = din("sgu_vb", [128, 2 * D])
    sgu_wsT = din("sgu_wsT", [128, 8, 128])
    sgu_bs2 = din("sgu_bs2", [2, 8 * 128])
    sel2 = din("sel2", [2, 2])
    sgu_w_out = din("sgu_w_out", [2 * D, D])
    ffn_w_up = din("ffn_w_up", [2, D, 2 * DFF])
    ffn_cw = din("ffn_cw", [2, 128, 4, NFC])
    ffn_w_down = din("ffn_w_down", [2, DFF, D])
    out_d = kb.dram("out", [OWN, D], F32, kind="ExternalOutput")

    KT_d = kb.dram("KT_d", [D, S], BF16)
    VS_d = kb.dram("VS_d", [S, D], BF16)
    QT_d = kb.dram("QT_d", [D, EXT], BF16)
    SG_d = kb.dram("SG_d", [D, EXT], BF16)
    FK_d = kb.dram("FK_d", [H, 3, S], BF16)
    CQ_d = kb.dram("CQ_d", [H, 3, EXT], BF16)
    NLF_d = kb.dram("NLF_d", [H, S], F32)
    GB_d = kb.dram("GB_d", [4, 128, D], F32)
    AT_d = kb.dram("AT_d", [D, EXT], BF16)
    XA_d = kb.dram("XA_d", [EXT, D], F32)
    XB_d = kb.dram("XB_d", [EXT, D], F32)
    HT_d = kb.dram("HT_d", [D, EXT], BF16)
    UT_d = kb.dram("UT_d", [DFF, EXT], BF16)
    YT_d = kb.dram("YT_d", [2 * D, EXT], BF16)

    modT = kb.sb("modT", [128, 96], F32, persist=True)
    gmT = kb.sb("gmT", [128, 4 * KC], F32, persist=True)
    ident = kb.sb("ident", [128, 128], F32, persist=True)
    identb = kb.sb("identb", [128, 128], BF16, persist=True)
    onesf = kb.sb("onesf", [128, 128], F32, persist=True)
    cst = kb.sb("cst", [128, 8], F32, persist=True)
    onesb = kb.sb("onesb", [128, 128], BF16, persist=True)

    pbig = kb.ps("pbig", [128, 4096], F32)

    class PV:
        def __init__(self, lo, n, name):
            self.t = T(kb, None, name)
            self.lo = lo
            self.n = n
            kb.all_tiles.append(self.t)

        def ap(self, p0=0, p1=128, c0=0, c1=None):
            c1 = self.n if c1 is None else c1
            return pbig.h[p0:p1, self.lo + c0:self.lo + c1]

    kb.op(POOL, lambda e: e.memset(cst.h[:, 0:1], EPS), w=[cst])
    kb.op(POOL, lambda e: e.memset(cst.h[:, 1:2], 64 * EPS), w=[cst])
    kb.op(POOL, lambda e: e.memset(cst.h[:, 2:3], 1.0), w=[cst])
    kb.op(POOL, lambda e: e.memset(cst.h[:, 3:4], 1e-30), w=[cst])
    kb.op(POOL, lambda e: e.memset(cst.h[:, 4:5], 0.0), w=[cst])
    kb.op(POOL, lambda e: e.memset(onesb.h[:], 1.0), w=[onesb])
    kb.op(POOL, lambda e: e.memset(onesf.h[:], 1.0), w=[onesf])
    kb.op(POOL, lambda e: e.affine_select(out=ident.h[:], in_=onesf.h[:], pattern=[[-1, 128]],
                                          compare_op=ALU.is_equal, fill=0.0, base=0, channel_multiplier=1),
          r=[onesf], w=[ident])
    kb.op(POOL, lambda e: e.tensor_copy(out=identb.h[:], in_=ident.h[:]), r=[ident], w=[identb])

    def stage_adaln():
        kb.stage_begin()
        c_sb = kb.sb("c_sb", [128, KC], F32)
        c_act = kb.sb("c_act", [128, KC], F32)
        crep = kb.sb("crep", [128, KC, 128], F32)
        abT = kb.sb("abT", [128, 96], F32)
        abg = kb.sb("abg", [128, 4 * D], F32)
        ng = kb.sb("ng", [128, 4 * KC], F32)
        wts = [kb.sb(f"adw{i}", [128, KC, 512], F32) for i in range(4)]
        pcol = PV(0, 512, "pcol")
        pg = [PV(512 + 512 * i, 512, f"pg{i}") for i in range(4)]
        gbc = kb.sb("gbc", [128, 4, D], F32)
        kb.dma(SP, c_sb.h[:], cvec.h[:, :], r=[cvec], w=[c_sb], owner=c_sb)
        kb.dma(SP, abT.h[:], ada_bT.h[:, :], r=[ada_bT], w=[abT], owner=abT)
        kb.dma(SP, abg.h[:], ada_bg.h[:, :], r=[ada_bg], w=[abg], owner=abg)
        kb.dma(SP, ng.h[:], ngT.h[:, :], r=[ngT], w=[ng], owner=ng)
        kb.op(ACT, lambda e: e.activation(out=c_act.h[:], in_=c_sb.h[:], func=AF.Silu), r=[c_sb], w=[c_act])
        for k in range(KC):
            kb.op(DVE, lambda e, k=k: e.tensor_copy(out=crep.h[:, k, :], in_=c_act.h[:, k:k + 1].to_broadcast([128, 128])),
                  r=[c_act], w=[crep])
        it = 0
        for l in range(2):
            for rr in range(12):
                wt = wts[it % 4]
                it += 1
                src = ada_w.h[l, :, rr * 512:(rr + 1) * 512].rearrange("(k p) n -> p k n", p=128)
                kb.dma(SP if it % 2 else ACT, wt.h[:], src, r=[ada_w], w=[wt], owner=wt)
                for j in range(4):
                    col = l * 48 + rr * 4 + j
                    kb.mm([lambda e, k=k, j=j, col=col, wt=wt: e.matmul(pcol.ap(c0=col, c1=col + 1), lhsT=wt.h[:, k, j * 128:(j + 1) * 128],
                                                                     rhs=c_act.h[:, k:k + 1], start=(k == 0), stop=(k == KC - 1))
                           for k in range(KC)], r=[wt, c_act], w=[pcol.t])
                gsel = {4: 0, 5: 1, 10: 2, 11: 3}.get(rr)
                if gsel is not None:
                    pgt = pg[gsel]
                    kb.mm([lambda e, k=k, wt=wt, pgt=pgt: e.matmul(pgt.ap(), lhsT=crep.h[:, k, :], rhs=wt.h[:, k, :],
                                                                  start=(k == 0), stop=(k == KC - 1))
                           for k in range(KC)], r=[wt, crep], w=[pgt.t])
            for gi in range(4):
                w_ = gi // 2
                hf = gi % 2
                kb.op(DVE, lambda e, gi=gi, w_=w_, hf=hf, l=l: e.tensor_tensor(
                    out=gbc.h[:, l * 2 + w_, hf * 512:(hf + 1) * 512], in0=pg[gi].ap(),
                    in1=abg.h[:, (l * 2 + w_) * D + hf * 512:(l * 2 + w_) * D + (hf + 1) * 512], op=ALU.add),
                    r=[pg[gi].t, abg], w=[gbc])
        for gi4 in range(4):
            kb.dma(POOL, GB_d.h[gi4], gbc.h[:, gi4, :], r=[gbc], w=[GB_d], owner=gbc)
        kb.op(DVE, lambda e: e.tensor_tensor(out=modT.h[:], in0=pcol.ap(c1=96), in1=abT.h[:], op=ALU.add),
              r=[pcol.t, abT], w=[modT])
        for l in range(2):
            for w_ in range(2):
                sc0 = l * 48 + (1 + 3 * w_) * 8
                kb.op(DVE, lambda e, l=l, w_=w_, sc0=sc0: e.scalar_tensor_tensor(
                    out=gmT.h[:, (l * 2 + w_) * 8:(l * 2 + w_) * 8 + 8], in0=modT.h[:, sc0:sc0 + 8], scalar=1.0,
                    in1=ng.h[:, (l * 2 + w_) * 8:(l * 2 + w_) * 8 + 8], op0=ALU.add, op1=ALU.mult),
                    r=[modT, ng], w=[gmT])
        kb.stage_end()

    def shcol(l, w_, k):
        c = l * 48 + (3 * w_) * 8 + k
        return modT.h[:, c:c + 1]

    def gmcol(l, w_, k):
        c = (l * 2 + w_) * 8 + k
        return gmT.h[:, c:c + 1]

    class NormCtx:
        def __init__(self, tag, need_xn=True, nbuf=2):
            self.junk = kb.sb("junk" + tag, [128, D], BF16)
            self.ss = [kb.sb(f"ss{tag}{i}", [128, 4], F32) for i in range(2)]
            self.rs = [kb.sb(f"rs{tag}{i}", [128, 4], F32) for i in range(2)]
            self.xn = ([kb.sb(f"xn{tag}{i}", [128, 4, D], BF16) for i in range(nbuf)] * (2 // nbuf)) if need_xn else None
            self.it = 0

    def rstd_of(ctx, xt, nb):
        ss = ctx.ss[ctx.it % 2]
        rs = ctx.rs[ctx.it % 2]
        for j in range(nb):
            kb.op(ACT, lambda e, j=j: e.activation(out=ctx.junk.h[:], in_=xt.h[:, j, :], func=AF.Square,
                                                   accum_out=ss.h[:, j:j + 1]), r=[xt], w=[ctx.junk, ss])
        kb.op(ACT, lambda e: e.activation(out=rs.h[:, 0:nb], in_=ss.h[:, 0:nb], func=AF.Ln, scale=1.0 / D, bias=cst.h[:, 0:1]),
              r=[ss, cst], w=[rs])
        kb.op(ACT, lambda e: e.activation(out=rs.h[:, 0:nb], in_=rs.h[:, 0:nb], func=AF.Exp, scale=-0.5), r=[rs], w=[rs])
        return rs

    ptr = [PV(0, 512, "ptr0"), PV(512, 512, "ptr1")]

    def ptr_ap(i, n):
        return pbig.h[:, i * 512:(i + 1) * 512].bitcast(BF16)[:, 0:n]

    def norm_to_hT(ctx, xt, nb, l, w_, hT, evac_flip=[0]):
        rs = rstd_of(ctx, xt, nb)
        xn = ctx.xn[ctx.it % 2]
        ctx.it += 1
        for j in range(nb):
            kb.op(DVE, lambda e, j=j: e.tensor_scalar(out=xn.h[:, j, :], in0=xt.h[:, j, :], scalar1=rs.h[:, j:j + 1], scalar2=None,
                                                      op0=ALU.mult), r=[xt, rs], w=[xn])
        n = nb * 128
        for k in range(KC):
            pi = k % 2
            kb.mm([lambda e, j=j, k=k, pi=pi: e.transpose(ptr_ap(pi, 512)[:, j * 128:(j + 1) * 128], xn.h[:, j, k * 128:(k + 1) * 128], identb.h[:])
                   for j in range(nb)], r=[xn, identb], w=[ptr[pi].t])
            if evac_flip[0] % 2 == 0:
                kb.op(ACT, lambda e, k=k, pi=pi: e.activation(out=hT.h[:, k, 0:n], in_=ptr_ap(pi, n), func=AF.Identity,
                                                              scale=gmcol(l, w_, k), bias=shcol(l, w_, k)),
                      r=[ptr[pi].t, gmT, modT], w=[hT])
            else:
                kb.op(DVE, lambda e, k=k, pi=pi: e.tensor_scalar(out=hT.h[:, k, 0:n], in0=ptr_ap(pi, n), scalar1=gmcol(l, w_, k),
                                                                 scalar2=shcol(l, w_, k), op0=ALU.mult, op1=ALU.add),
                      r=[ptr[pi].t, gmT, modT], w=[hT])
            evac_flip[0] += 1

    def load_w_bf16(name, src_ap, kchunks, ncols, src_t=None):
        wt = kb.sb(name, [128, kchunks, ncols], BF16)
        for k in range(kchunks):
            if src_t is None:
                kb.dma(POOL, wt.h[:, k, :], src_ap[k * 128:(k + 1) * 128, :], r=[], w=[wt], owner=wt)
            else:
                kb.dma(SP if k % 2 == 0 else ACT, wt.h[:, k, :], src_ap[k * 128:(k + 1) * 128, :], r=[src_t], w=[wt], owner=wt)
        return wt

    WB = {
        "out": kb.dram("WB_out", [D, D], BF16),
        "up0": kb.dram("WB_up0", [D, 2 * DFF], BF16),
        "up1": kb.dram("WB_up1", [D, 2 * DFF], BF16),
        "dn0": kb.dram("WB_dn0", [DFF, D], BF16),
        "dn1": kb.dram("WB_dn1", [DFF, D], BF16),
        "sgi": kb.dram("WB_sgi", [D, 4 * D], BF16),
        "sgo": kb.dram("WB_sgo", [2 * D, D], BF16),
    }
    WSRC = {"out": fox_w_out.h, "up0": ffn_w_up.h[0], "up1": ffn_w_up.h[1], "dn0": ffn_w_down.h[0], "dn1": ffn_w_down.h[1],
            "sgi": sgu_w_in.h, "sgo": sgu_w_out.h}

    def prefetch_w(key, kchunks, ncols, side, cast_src=None, defer=False):
        kb.uid += 1
        g = nc.sbuf_tensor(f"W{key}_{kb.uid}", [128, kchunks, ncols], BF16, side=side)
        wt = T(kb, g.__enter__(), "W" + key)
        kb.all_tiles.append(wt)
        def issue():
            for k in range(kchunks):
                if cast_src is not None:
                    kb.dma(POOL, wt.h[:, k, :], cast_src[k * 128:(k + 1) * 128, :], r=[], w=[wt], owner=wt)
                else:
                    kb.dma(SP if k % 2 == 0 else ACT, wt.h[:, k, :], WB[key].h[k * 128:(k + 1) * 128, :], r=[WB[key]], w=[wt], owner=wt)
        if defer:
            return (wt, g, issue)
        issue()
        return (wt, g, None)

    def free_w(slot):
        slot[1].__exit__(None, None, None)

    def conversion_jobs():
        jobs = []
        for key in ("up0", "dn0", "sgi", "sgo", "up1", "dn1"):
            dst = WB[key]
            src = WSRC[key]
            rows = dst.h.shape[0]
            for r0 in range(0, rows, 256):
                r1 = min(rows, r0 + 256)
                jobs.append(lambda dst=dst, src=src, r0=r0, r1=r1: kb.dma(POOL, dst.h[r0:r1, :], src[r0:r1, :], r=[], w=[dst], owner=dst))
        return jobs

    def stage_proj0(w_in):
        kb.stage_begin()
        ctx = NormCtx("p")
        xts = [kb.sb(f"xt{i}", [128, 4, D], F32) for i in range(2)]
        hTs = [kb.sb(f"hT{i}", [128, KC, 512], BF16) for i in range(2)]
        nlt = [kb.sb(f"nlt{i}", [H, 512], F32) for i in range(2)]
        bd = kb.sb("bd", [128, 128], BF16)
        qg8 = kb.sb("qg8", [128, 1], F32)
        kg1 = kb.sb("kg1", [128, 1], F32)
        nbf = kb.sb("nbf", [H, 1], F32)
        sq = [kb.sb(f"sq{i}", [128, 512], BF16) for i in range(2)]
        rq = [kb.sb(f"rq{i}", [128, 512], F32) for i in range(2)]
        kq = [kb.sb(f"kq{i}", [128, 512], BF16) for i in range(3)]
        vt = [kb.sb(f"vt{i}", [128, 4, D], BF16) for i in range(2)]
        ef = [kb.sb(f"ef{i}", [H, 512], F32) for i in range(2)]
        pmm = [PV(1024 + 512 * i, 512, f"pmm{i}") for i in range(4)]
        pss = [PV(3072 + 512 * i, 512, f"pss{i}") for i in range(2)]
        kb.op(POOL, lambda e: e.memset(bd.h[:], 0.0), w=[bd])
        kb.op(POOL, lambda e: e.memset(bd.h[0:64, 0:64], 1.0), w=[bd])
        kb.op(POOL, lambda e: e.memset(bd.h[64:128, 64:128], 1.0), w=[bd])
        kb.dma(SP, qg8.h[:], fox_qg.h[:, :], r=[fox_qg], w=[qg8], owner=qg8)
        kb.dma(SP, kg1.h[:], fox_kg.h[:, :], r=[fox_kg], w=[kg1], owner=kg1)
        kb.dma(SP, nbf.h[:], fox_bf.h[:, :], r=[fox_bf], w=[nbf], owner=nbf)
        kb.op(DVE, lambda e: e.tensor_scalar(out=qg8.h[:], in0=qg8.h[:], scalar1=8.0, scalar2=None, op0=ALU.mult), r=[qg8], w=[qg8])
        kb.op(DVE, lambda e: e.tensor_scalar(out=nbf.h[:], in0=nbf.h[:], scalar1=-1.0, scalar2=None, op0=ALU.mult), r=[nbf], w=[nbf])
        pmi = [0]
        late = [None]

        def norm_tail(p, sqt, rqt, kqt, pst, kind, hp, tt):
            kb.mm([lambda e: e.matmul(pst.ap(), lhsT=bd.h[:], rhs=sqt.h[:], start=True, stop=True)], r=[bd, sqt], w=[pst.t])
            kb.op(ACT, lambda e: e.activation(out=rqt.h[:], in_=pst.ap(), func=AF.Ln, bias=cst.h[:, 1:2]), r=[pst.t, cst], w=[rqt])
            kb.op(ACT, lambda e: e.activation(out=rqt.h[:], in_=rqt.h[:], func=AF.Exp, scale=-0.5), r=[rqt], w=[rqt])
            gsc = kg1 if kind == "k" else qg8
            kb.op(DVE, lambda e: e.scalar_tensor_tensor(out=kqt.h[:], in0=p.ap(), scalar=gsc.h[:, 0:1], in1=rqt.h[:], op0=ALU.mult, op1=ALU.mult),
                  r=[p.t, rqt, gsc], w=[kqt])
            if kind == "k":
                kb.dma(POOL, KT_d.h[hp * 128:(hp + 1) * 128, tt * 512:(tt + 1) * 512], kqt.h[:], r=[kqt], w=[KT_d], owner=kqt)
            elif tt == 7:
                kb.dma(POOL, QT_d.h[hp * 128:(hp + 1) * 128, 0:256], kqt.h[:, 256:512], r=[kqt], w=[QT_d], owner=kqt)
            else:
                q0 = 256 + (tt - 8) * 512
                kb.dma(POOL, QT_d.h[hp * 128:(hp + 1) * 128, q0:q0 + 512], kqt.h[:], r=[kqt], w=[QT_d], owner=kqt)

        def next_pmm():
            p = pmm[pmi[0] % 4]
            pmi[0] += 1
            return p
        cnt = [0]
        for tt in range(16):
            xt = xts[tt % 2]
            hT = hTs[tt % 2]
            own = tt >= 7
            kb.dma(SP, xt.h[:], xv.h[tt * 512:(tt + 1) * 512, :].rearrange("(j p) d -> p j d", p=128), r=[xv], w=[xt], owner=xt)
            norm_to_hT(ctx, xt, 4, 0, 0, hT)
            jobs = [("k", hp) for hp in range(8)] + ([("q", hp) for hp in range(8)] if own else [])
            for kind, hp in jobs:
                c0 = (D if kind == "k" else 0) + hp * 128
                p = next_pmm()
                kb.mm([lambda e, k=k, c0=c0, p=p: e.matmul(p.ap(), lhsT=w_in.h[:, k, c0:c0 + 128], rhs=hT.h[:, k, :],
                                                         start=(k == 0), stop=(k == KC - 1)) for k in range(KC)],
                      r=[w_in, hT], w=[p.t])
                i = cnt[0]
                cnt[0] += 1
                sqt = sq[i % 2]
                rqt = rq[i % 2]
                kqt = kq[i % 3]
                pst = pss[i % 2]
                kb.op(ACT, lambda e, p=p, sqt=sqt: e.activation(out=sqt.h[:], in_=p.ap(), func=AF.Square), r=[p.t], w=[sqt])
                if late[0] is not None:
                    late[0]()
                late[0] = (lambda p=p, sqt=sqt, rqt=rqt, kqt=kqt, pst=pst, kind=kind, hp=hp, tt=tt: norm_tail(p, sqt, rqt, kqt, pst, kind, hp, tt))
            if late[0] is not None:
                late[0]()
                late[0] = None
            vtt = vt[tt % 2]
            for j in range(4):
                for hf in range(2):
                    p = next_pmm()
                    kb.mm([lambda e, k=k, j=j, hf=hf, p=p: e.matmul(p.ap(), lhsT=hT.h[:, k, j * 128:(j + 1) * 128],
                                                                   rhs=w_in.h[:, k, 2 * D + hf * 512:2 * D + (hf + 1) * 512],
                                                                   start=(k == 0), stop=(k == KC - 1)) for k in range(KC)],
                          r=[w_in, hT], w=[p.t])
                    if (j + hf) % 2 == 0:
                        kb.op(DVE, lambda e, j=j, hf=hf, p=p: e.tensor_copy(out=vtt.h[:, j, hf * 512:(hf + 1) * 512], in_=p.ap()),
                              r=[p.t], w=[vtt])
                    else:
                        kb.op(ACT, lambda e, j=j, hf=hf, p=p: e.activation(out=vtt.h[:, j, hf * 512:(hf + 1) * 512], in_=p.ap(), func=AF.Identity),
                              r=[p.t], w=[vtt])
            kb.dma(POOL, VS_d.h[tt * 512:(tt + 1) * 512, :].rearrange("(j p) d -> p j d", p=128), vtt.h[:], r=[vtt], w=[VS_d], owner=vtt)
            p = next_pmm()
            kb.mm([lambda e, k=k, p=p: e.matmul(p.ap(p1=H), lhsT=w_in.h[:, k, 4 * D:4 * D + H], rhs=hT.h[:, k, :],
                                               start=(k == 0), stop=(k == KC - 1)) for k in range(KC)],
                  r=[w_in, hT], w=[p.t])
            eft = ef[tt % 2]
            kb.op(ACT, lambda e, p=p, eft=eft: e.activation(out=eft.h[:], in_=p.ap(p1=H), func=AF.Exp, scale=-1.0, bias=nbf.h[:, 0:1]),
                  r=[p.t, nbf], w=[eft])
            nl = nlt[tt % 2]
            kb.op(ACT, lambda e, eft=eft, nl=nl: e.activation(out=nl.h[:], in_=eft.h[:], func=AF.Ln, bias=cst.h[0:H, 2:3]),
                  r=[eft, cst], w=[nl])
            kb.dma(POOL, NLF_d.h[:, tt * 512:(tt + 1) * 512], nl.h[:], r=[nl], w=[NLF_d], owner=nl)
            if own:
                for oc in range(8):
                    p = next_pmm()
                    kb.mm([lambda e, k=k, oc=oc, p=p: e.matmul(p.ap(), lhsT=w_in.h[:, k, 3 * D + oc * 128:3 * D + (oc + 1) * 128],
                                                              rhs=hT.h[:, k, :], start=(k == 0), stop=(k == KC - 1)) for k in range(KC)],
                          r=[w_in, hT], w=[p.t])
                    i = cnt[0]
                    cnt[0] += 1
                    kqt = kq[i % 3]
                    kb.op(ACT, lambda e, p=p, kqt=kqt: e.activation(out=kqt.h[:], in_=p.ap(), func=AF.Sigmoid), r=[p.t], w=[kqt])
                    if tt == 7:
                        kb.dma(POOL, SG_d.h[oc * 128:(oc + 1) * 128, 0:256], kqt.h[:, 256:512], r=[kqt], w=[SG_d], owner=kqt)
                    else:
                        q0 = 256 + (tt - 8) * 512
                        kb.dma(POOL, SG_d.h[oc * 128:(oc + 1) * 128, q0:q0 + 512], kqt.h[:], r=[kqt], w=[SG_d], owner=kqt)
        kb.stage_end()

    def stage_decay():
        kb.stage_begin()
        nls = [kb.sb(f"nls{i}", [H, 2048], F32) for i in range(2)]
        ones16 = kb.sb("ones16", [H, 2048], F32)
        nf = [kb.sb(f"nf{i}", [H, 2048], F32) for i in range(2)]
        kp = [kb.sb(f"kp{i}", [H, 2048], F32) for i in range(2)]
        r1 = kb.sb("r1", [H, 2048], F32)
        parts = [[kb.sb(f"part{i}_{j}", [H, 2048], BF16) for j in range(3)] for i in range(2)]
        qn = kb.sb("qn", [H, 2048], F32)
        qparts = [[kb.sb(f"qpart{i}_{j}", [H, 2048], BF16) for j in range(3)] for i in range(2)]
        kb.op(POOL, lambda e: e.memset(ones16.h[:], 1.0), w=[ones16])
        for ch in range(4):
            a = nf[ch % 2]
            kpt = kp[ch % 2]
            kb.dma(SP, kpt.h[:], kpad.h[:, ch * 2048:(ch + 1) * 2048], r=[kpad], w=[kpt], owner=kpt)
            nlf = nls[ch % 2]
            kb.dma(SP, nlf.h[:], NLF_d.h[:, ch * 2048:(ch + 1) * 2048], r=[NLF_d], w=[nlf], owner=nlf)
            if ch == 0:
                kb.op(DVE, lambda e, a=a: e.tensor_tensor_scan(out=a.h[:], data0=ones16.h[:], data1=nlf.h[:], initial=0.0,
                                                               op0=ALU.mult, op1=ALU.add), r=[ones16, nlf], w=[a])
            else:
                prev = nf[(ch - 1) % 2]
                kb.op(DVE, lambda e, a=a, prev=prev, ch=ch: e.tensor_tensor_scan(out=a.h[:], data0=ones16.h[:], data1=nlf.h[:],
                                                                                 initial=prev.h[:, 2047:2048], op0=ALU.mult, op1=ALU.add),
                      r=[ones16, nlf, prev], w=[a])
            def split3(src, dst3, n):
                kb.op(DVE, lambda e: e.tensor_copy(out=dst3[0].h[:, 0:n], in_=src.h[:, 0:n]), r=[src], w=[dst3[0]])
                kb.op(DVE, lambda e: e.tensor_tensor(out=r1.h[:, 0:n], in0=src.h[:, 0:n], in1=dst3[0].h[:, 0:n], op=ALU.subtract), r=[src, dst3[0]], w=[r1])
                kb.op(DVE, lambda e: e.tensor_copy(out=dst3[1].h[:, 0:n], in_=r1.h[:, 0:n]), r=[r1], w=[dst3[1]])
                kb.op(DVE, lambda e: e.tensor_tensor(out=r1.h[:, 0:n], in0=r1.h[:, 0:n], in1=dst3[1].h[:, 0:n], op=ALU.subtract), r=[r1, dst3[1]], w=[r1])
                kb.op(DVE, lambda e: e.tensor_copy(out=dst3[2].h[:, 0:n], in_=r1.h[:, 0:n]), r=[r1], w=[dst3[2]])
            if ch >= 1:
                c0 = 1792 if ch == 1 else 0
                n = 2048 - c0
                e0 = 0 if ch == 1 else 256 + (ch - 2) * 2048
                kb.op(DVE, lambda e: e.tensor_scalar(out=qn.h[:, 0:n], in0=a.h[:, c0:2048], scalar1=-1.0, scalar2=None, op0=ALU.mult), r=[a], w=[qn])
                qp = qparts[ch % 2]
                split3(qn, qp, n)
                for j in range(3):
                    kb.dma(POOL, CQ_d.h[:, j, e0:e0 + n], qp[j].h[:, 0:n], r=[qp[j]], w=[CQ_d], owner=qp[j])
            kb.op(DVE, lambda e, a=a, kpt=kpt: e.tensor_tensor(out=kpt.h[:], in0=a.h[:], in1=kpt.h[:], op=ALU.add), r=[a, kpt], w=[kpt])
            pt = parts[ch % 2]
            split3(kpt, pt, 2048)
            for j in range(3):
                kb.dma(POOL, FK_d.h[:, j, ch * 2048:(ch + 1) * 2048], pt[j].h[:], r=[pt[j]], w=[FK_d], owner=pt[j])
        kb.stage_end()

    def stage_attn(pre=None):
        kb.stage_begin()
        nxt_slot = pre() if pre else None
        KA = [kb.sb(f"KA{i}", [70, S], BF16) for i in range(2)]
        QA = [kb.sb(f"QA{i}", [70, EXT], BF16) for i in range(2)]
        VA = [kb.sb(f"VA{i}", [128, 64, 65], BF16) for i in range(2)]
        SGh = [kb.sb(f"SGh{i}", [64, EXT], BF16) for i in range(2)]
        maskf = kb.sb("maskf", [128, 128], F32)
        maskb = kb.sb("maskb", [128, 128], BF16)
        pT = [kb.sb(f"pT{i}", [128, 1024], BF16) for i in range(3)]
        accs = [kb.sb(f"accs{i}", [65, 512], F32) for i in range(2)]
        rden = [kb.sb(f"rden{i}", [65, 512], F32) for i in range(2)]
        rhi = [kb.sb(f"rhi{i}", [65, 512], BF16) for i in range(2)]
        rlo = [kb.sb(f"rlo{i}", [65, 512], BF16) for i in range(2)]
        gs = [kb.sb(f"gs{i}", [64, 512], F32) for i in range(2)]
        ag = [kb.sb(f"ag{i}", [64, 512], BF16) for i in range(2)]
        psS = [PV(0, 1024, "psS0"), PV(1024, 1024, "psS1"), PV(2048, 1024, "psS2")]
        acc = [PV(3072, 512, "acc0")] * 2
        pbc = [PV(3584, 512, "pbc0")] * 2
        kb.dma(SP, maskf.h[:], maskb_d.h[:, :], r=[maskb_d], w=[maskf], owner=maskf)
        kb.op(DVE, lambda e: e.tensor_copy(out=maskb.h[:], in_=maskf.h[:]), r=[maskf], w=[maskb])
        conv = conversion_jobs()
        for i in range(2):
            kb.op(POOL, lambda e, i=i: e.memset(KA[i].h[64:70, :], 1.0), w=[KA[i]])
            kb.op(POOL, lambda e, i=i: e.memset(QA[i].h[64:70, :], 1.0), w=[QA[i]])
            kb.op(POOL, lambda e, i=i: e.memset(VA[i].h[:, :, 64:65], 1.0), w=[VA[i]])
        def load_head(h):
            ka, qa, va, sg = KA[h % 2], QA[h % 2], VA[h % 2], SGh[h % 2]
            kb.dma(SP, ka.h[0:64, :], KT_d.h[h * 64:(h + 1) * 64, :], r=[KT_d], w=[ka], owner=ka)
            kb.dma(SP, ka.h[64:67, :], FK_d.h[h, :, :], r=[FK_d], w=[ka], owner=ka)
            kb.dma(SP, qa.h[0:64, :], QT_d.h[h * 64:(h + 1) * 64, :], r=[QT_d], w=[qa], owner=qa)
            kb.dma(SP, qa.h[67:70, :], CQ_d.h[h, :, :], r=[CQ_d], w=[qa], owner=qa)
            for c4 in range(4):
                kb.dma(SP, va.h[:, c4 * 16:(c4 + 1) * 16, 0:64],
                       VS_d.h[c4 * 2048:(c4 + 1) * 2048, h * 64:(h + 1) * 64].rearrange("(kb p) d -> p kb d", p=128),
                       r=[VS_d], w=[va], owner=va)
            kb.dma(SP, sg.h[:], SG_d.h[h * 64:(h + 1) * 64, :], r=[SG_d], w=[sg], owner=sg)

        groups = []
        for h in range(H):
            for qi, (t0, wd) in enumerate(TILES):
                qb0 = (E0 + t0) // 128
                nkb = qb0 + wd // 128
                for g0 in range(0, nkb, 2):
                    groups.append((h, qi, t0, wd, qb0, nkb, list(range(g0, min(g0 + 2, nkb)))))

        def emit_qk(gidx):
            h, qi, t0, wd, qb0, nkb, grp = groups[gidx]
            ka, qa = KA[h % 2], QA[h % 2]
            ps_ = psS[gidx % 3]
            SW = wd
            fns = []
            for sl, kbi in enumerate(grp):
                di = kbi - qb0
                c0 = 128 * di if di > 0 else 0
                fns.append(lambda e, sl=sl, kbi=kbi, c0=c0, di=di: e.matmul(
                    ps_.ap(c0=sl * SW + c0, c1=sl * SW + wd), lhsT=ka.h[0:70, kbi * 128:(kbi + 1) * 128],
                    rhs=qa.h[0:70, t0 + c0:t0 + wd], start=True, stop=(di < 0), skip_group_check=True))
                if di >= 0:
                    fns.append(lambda e, sl=sl, di=di: e.matmul(
                        ps_.ap(c0=sl * SW + 128 * di, c1=sl * SW + 128 * di + 128), lhsT=identb.h[:], rhs=maskb.h[:],
                        start=False, stop=True, skip_group_check=True))
            kb.mm(fns, r=[ka, qa, identb, maskb], w=[ps_.t])

        def emit_exp_pv(gidx):
            h, qi, t0, wd, qb0, nkb, grp = groups[gidx]
            va, sg = VA[h % 2], SGh[h % 2]
            ps_ = psS[gidx % 3]
            pt_ = pT[gidx % 3]
            ei = h * len(TILES) + qi
            accv = acc[ei % 2]
            SW = wd
            ncol = (len(grp) - 1) * SW + wd
            d0 = grp[0] - qb0
            cs = 128 * d0 if d0 > 0 else 0
            kb.op(ACT, lambda e: e.activation(out=pt_.h[:, cs:ncol], in_=ps_.ap(c0=cs, c1=ncol), func=AF.Exp), r=[ps_.t], w=[pt_])
            fns = []
            for sl, kbi in enumerate(grp):
                di = kbi - qb0
                c0 = 128 * di if di > 0 else 0
                fns.append(lambda e, sl=sl, kbi=kbi, c0=c0: e.matmul(
                    accv.ap(p1=65, c0=c0, c1=wd), lhsT=va.h[:, kbi, 0:65], rhs=pt_.h[:, sl * SW + c0:sl * SW + wd],
                    start=(kbi == 0), stop=(kbi == nkb - 1), skip_group_check=True))
            kb.mm(fns, r=[va, pt_], w=[accv.t])
            if grp[-1] != nkb - 1:
                return
            e2 = ei % 2
            rd, rh, rl, gst, agt, pb = rden[e2], rhi[e2], rlo[e2], gs[e2], ag[e2], pbc[e2]
            asb = accs[e2]
            kb.op(DVE, lambda e: e.tensor_copy(out=asb.h[0:65, 0:wd], in_=accv.ap(p1=65, c1=wd)), r=[accv.t], w=[asb])
            kb.op(DVE, lambda e: e.tensor_scalar(out=rd.h[64:65, 0:wd], in0=asb.h[64:65, 0:wd], scalar1=cst.h[64:65, 3:4],
                                                 scalar2=None, op0=ALU.add), r=[asb, cst], w=[rd])
            kb.op(DVE, lambda e: e.reciprocal(out=rd.h[64:65, 0:wd], in_=rd.h[64:65, 0:wd]), r=[rd], w=[rd])
            kb.op(DVE, lambda e: e.tensor_copy(out=rh.h[64:65, 0:wd], in_=rd.h[64:65, 0:wd]), r=[rd], w=[rh])
            kb.op(DVE, lambda e: e.tensor_tensor(out=rd.h[64:65, 0:wd], in0=rd.h[64:65, 0:wd], in1=rh.h[64:65, 0:wd], op=ALU.subtract),
                  r=[rd, rh], w=[rd])
            kb.op(DVE, lambda e: e.tensor_copy(out=rl.h[64:65, 0:wd], in_=rd.h[64:65, 0:wd]), r=[rd], w=[rl])
            pending.append((gidx + 7, lambda: epilogue_b(h, t0, wd, asb, rh, rl, gst, agt, pb, sg)))

        def epilogue_b(h, t0, wd, asb, rh, rl, gst, agt, pb, sg):
            kb.mm([lambda e: e.matmul(pb.ap(p1=64, c1=wd), lhsT=onesb.h[64:65, 0:64], rhs=rh.h[64:65, 0:wd], start=True, stop=False),
                   lambda e: e.matmul(pb.ap(p1=64, c1=wd), lhsT=onesb.h[64:65, 0:64], rhs=rl.h[64:65, 0:wd], start=False, stop=True)],
                  r=[onesb, rh, rl], w=[pb.t])
            kb.op(DVE, lambda e: e.tensor_tensor(out=gst.h[:, 0:wd], in0=pb.ap(p1=64, c1=wd), in1=sg.h[:, t0:t0 + wd], op=ALU.mult),
                  r=[pb.t, sg], w=[gst])
            kb.op(DVE, lambda e: e.tensor_tensor(out=agt.h[:, 0:wd], in0=asb.h[0:64, 0:wd], in1=gst.h[:, 0:wd], op=ALU.mult),
                  r=[asb, gst], w=[agt])
            kb.dma(POOL, AT_d.h[h * 64:(h + 1) * 64, t0:t0 + wd], agt.h[:, 0:wd], r=[agt], w=[AT_d], owner=agt)

        pending = []
        load_head(0)
        loaded = 0
        LAG = 2
        for gidx in range(len(groups) + LAG):
            if gidx < len(groups):
                emit_qk(gidx)
            if conv and gidx >= 30 and gidx % 40 == 0:
                conv.pop(0)()
            while pending and pending[0][0] <= gidx:
                pending.pop(0)[1]()
            if gidx >= LAG:
                emit_exp_pv(gidx - LAG)
                hcur = groups[gidx - LAG][0]
                if hcur == loaded and hcur + 1 < H and not pending and (gidx < 12 or groups[gidx - 12][0] == hcur):
                    load_head(hcur + 1)
                    loaded = hcur + 1
        while pending:
            pending.pop(0)[1]()
        while conv:
            conv.pop(0)()
        kb.stage_end()
        return nxt_slot

    def stage_out(tag, inT_d, kcin, W, gidx, x_in_ap, x_in_t, x_out, nxt, final=False, pre=None, in_bufs=2):
        kb.stage_begin()
        nxt_slot = pre() if pre else None
        ctx = NormCtx(tag, need_xn=not final, nbuf=1)
        ins = [kb.sb(f"in{tag}{i}", [128, kcin, 512], BF16) for i in range(in_bufs)] * (2 // in_bufs)
        xts = [kb.sb(f"xo{tag}{i}", [128, 4, D], F32) for i in range(2)]
        tmp = [kb.sb(f"tmp{tag}{i}", [128, 512], F32) for i in range(2)]
        gb = kb.sb("gb" + tag, [128, D], F32)
        kb.dma(SP, gb.h[:], GB_d.h[gidx], r=[GB_d], w=[gb], owner=gb)
        py = [PV(1024 + 512 * i, 512, f"py{tag}{i}") for i in range(4)]
        if final:
            fg = kb.sb("fg", [128, D], F32)
            kb.dma(SP, fg.h[:], final_g.h[:, :], r=[final_g], w=[fg], owner=fg)
        else:
            hTs = [kb.sb(f"hTo{tag}", [128, KC, 512], BF16)] * 2
        itc = [0]

        def part1(ti):
            t0, wd = TILES[ti]
            nb = wd // 128
            it_ = ins[ti % 2]
            xt = xts[ti % 2]
            kb.dma(SP, it_.h[:, :, 0:wd], inT_d.h[:, t0:t0 + wd].rearrange("(k p) t -> p k t", p=128), r=[inT_d], w=[it_], owner=it_)
            kb.dma(SP, xt.h[:, 0:nb, :], x_in_ap[t0:t0 + wd, :].rearrange("(j p) d -> p j d", p=128), r=[x_in_t], w=[xt], owner=xt)
            for j in range(nb):
                for hf in range(2):
                    p = py[itc[0] % 4]
                    tm = tmp[itc[0] % 2]
                    itc[0] += 1
                    kb.mm([lambda e, k=k, j=j, hf=hf, p=p: e.matmul(p.ap(), lhsT=it_.h[:, k, j * 128:(j + 1) * 128],
                                                                   rhs=W.h[:, k, hf * 512:(hf + 1) * 512], start=(k == 0), stop=(k == kcin - 1))
                           for k in range(kcin)], r=[it_, W], w=[p.t])
                    kb.op(DVE, lambda e, p=p, tm=tm, hf=hf: e.tensor_tensor(out=tm.h[:], in0=p.ap(), in1=gb.h[:, hf * 512:(hf + 1) * 512], op=ALU.mult),
                          r=[p.t, gb], w=[tm])
                    kb.op(POOL, lambda e, tm=tm, j=j, hf=hf: e.tensor_tensor(out=xt.h[:, j, hf * 512:(hf + 1) * 512], in0=xt.h[:, j, hf * 512:(hf + 1) * 512],
                                                                            in1=tm.h[:], op=ALU.add), r=[tm, xt], w=[xt])
            if not final and x_out is not None:
                kb.dma(POOL, x_out.h[t0:t0 + wd, :].rearrange("(j p) d -> p j d", p=128), xt.h[:, 0:nb, :], r=[xt], w=[x_out], owner=xt)

        def part2(ti):
            t0, wd = TILES[ti]
            nb = wd // 128
            xt = xts[ti % 2]
            if final:
                if t0 + wd > HALO:
                    rs = rstd_of(ctx, xt, nb)
                    ctx.it += 1
                    o = xt
                    for j in range(nb):
                        kb.op(DVE, lambda e, j=j, o=o, rs=rs: e.scalar_tensor_tensor(out=o.h[:, j, :], in0=xt.h[:, j, :], scalar=rs.h[:, j:j + 1], in1=fg.h[:],
                                                                                     op0=ALU.mult, op1=ALU.mult), r=[xt, rs, fg], w=[o])
                    kb.dma(POOL, out_d.h[t0 - HALO:t0 - HALO + wd, :].rearrange("(j p) d -> p j d", p=128), o.h[:, 0:nb, :], r=[o], w=[out_d], owner=o)
            else:
                hT = hTs[ti % 2]
                l, w_ = nxt
                norm_to_hT(ctx, xt, nb, l, w_, hT)
                kb.dma(POOL, HT_d.h[:, t0:t0 + wd].rearrange("(k p) t -> p k t", p=128), hT.h[:, :, 0:wd], r=[hT], w=[HT_d], owner=hT)

        first = 1 if final else 0
        part1(first)
        if nxt_slot is not None and nxt_slot[2] is not None:
            nxt_slot[2]()
        for ti in range(first, len(TILES)):
            if ti + 1 < len(TILES):
                part1(ti + 1)
            part2(ti)
        kb.stage_end()
        return nxt_slot

    def stage_ffn_up(l, W, pre=None):
        kb.stage_begin()
        nxt_slot = pre() if pre else None
        cw = kb.sb("cw", [128, 4, NFC], F32)
        kb.dma(SP, cw.h[:], ffn_cw.h[l], r=[ffn_cw], w=[cw], owner=cw)
        tm = kb.sb("tmk", [128, HALO], F32)
        kb.dma(SP, tm.h[:], tmask.h[:, :], r=[tmask], w=[tm], owner=tm)
        hx = [kb.sb(f"hx{i}", [128, KC, 512], BF16) for i in range(2)]
        carry = [kb.sb(f"cc{i}", [128, NFC, 2], F32) for i in range(2)]
        Cg = [kb.sb(f"Cg{i}", [128, 512], F32) for i in range(3)]
        Cv = [kb.sb(f"Cv{i}", [128, 512], F32) for i in range(2)] + [None]
        ut = [kb.sb(f"ut{i}", [128, 22, 512], BF16) for i in range(2)]
        pa = [PV(512 * i, 512, f"pa{i}") for i in range(8)]
        kb.op(POOL, lambda e: e.memset(carry[0].h[:], 0.0), w=[carry[0]])
        fix = [kb.sb(f"fix{i}", [128, NFC, 2], F32) for i in range(2)]
        ftmp = kb.sb("ftmp", [128, NFC], F32)
        pi = 0
        ci = 0
        for ti, (t0, wd) in enumerate(TILES):
            hxt = hx[ti % 2]
            u = ut[ti % 2]
            cold, cnew = carry[ti % 2], carry[(ti + 1) % 2]
            kb.dma(SP, hxt.h[:, :, 0:wd], HT_d.h[:, t0:t0 + wd].rearrange("(k p) t -> p k t", p=128), r=[HT_d], w=[hxt], owner=hxt)
            if ti == 0:
                if nxt_slot is not None and nxt_slot[2] is not None:
                    nxt_slot[2]()
                for k in range(KC):
                    kb.op(DVE, lambda e, k=k: e.tensor_tensor(out=hxt.h[:, k, 0:HALO], in0=hxt.h[:, k, 0:HALO], in1=tm.h[:], op=ALU.mult),
                          r=[hxt, tm], w=[hxt])
            fx = fix[ti % 2]
            kb.op(POOL, lambda e: e.tensor_tensor(out=fx.h[:, :, 0], in0=cold.h[:, :, 0], in1=cw.h[:, 0, :], op=ALU.mult), r=[cold, cw], w=[fx])
            kb.op(POOL, lambda e: e.tensor_tensor(out=ftmp.h[:], in0=cold.h[:, :, 1], in1=cw.h[:, 1, :], op=ALU.mult), r=[cold, cw], w=[ftmp])
            kb.op(POOL, lambda e: e.tensor_tensor(out=fx.h[:, :, 0], in0=fx.h[:, :, 0], in1=ftmp.h[:], op=ALU.add), r=[fx, ftmp], w=[fx])
            kb.op(POOL, lambda e: e.tensor_tensor(out=fx.h[:, :, 1], in0=cold.h[:, :, 1], in1=cw.h[:, 0, :], op=ALU.mult), r=[cold, cw], w=[fx])
            for fp in range(22):
                cs = []
                for wi, fc in enumerate((fp, 22 + fp)):
                    p = pa[pi % 8]
                    pi += 1
                    kb.mm([lambda e, k=k, fc=fc, p=p: e.matmul(p.ap(c1=wd), lhsT=W.h[:, k, fc * 128:(fc + 1) * 128], rhs=hxt.h[:, k, 0:wd],
                                                              start=(k == 0), stop=(k == KC - 1)) for k in range(KC)], r=[W, hxt], w=[p.t])
                    C = Cg[ci % 3] if wi == 0 else Cv[ci % 2]
                    kb.op(ACT, lambda e, C=C, p=p, fc=fc: e.activation(out=C.h[:, 0:wd], in_=p.ap(c1=wd), func=AF.Identity,
                                                                       scale=cw.h[:, 2, fc:fc + 1], bias=cw.h[:, 3, fc:fc + 1]), r=[p.t, cw], w=[C])
                    kb.op(ACT, lambda e, p=p, fc=fc: e.activation(out=cnew.h[:, fc, :], in_=p.ap(c0=wd - 2, c1=wd), func=AF.Identity),
                          r=[p.t], w=[cnew])
                    kb.op(DVE, lambda e, C=C, p=p, fc=fc: e.scalar_tensor_tensor(out=C.h[:, 1:wd], in0=p.ap(c0=0, c1=wd - 1), scalar=cw.h[:, 1, fc:fc + 1],
                                                                                 in1=C.h[:, 1:wd], op0=ALU.mult, op1=ALU.add), r=[p.t, cw, C], w=[C])
                    kb.op(DVE, lambda e, C=C, p=p, fc=fc: e.scalar_tensor_tensor(out=C.h[:, 2:wd], in0=p.ap(c0=0, c1=wd - 2), scalar=cw.h[:, 0, fc:fc + 1],
                                                                                 in1=C.h[:, 2:wd], op0=ALU.mult, op1=ALU.add), r=[p.t, cw, C], w=[C])
                    kb.op(POOL, lambda e, C=C, fc=fc: e.tensor_tensor(out=C.h[:, 0:2], in0=C.h[:, 0:2], in1=fx.h[:, fc, :], op=ALU.add), r=[fx, C], w=[C])
                    cs.append(C)
                G = cs[0]
                ci += 1
                kb.op(ACT, lambda e, G=G: e.activation(out=G.h[:, 0:wd], in_=G.h[:, 0:wd], func=AF.Silu), r=[G], w=[G])
                kb.op(POOL, lambda e, G=G, c=cs[1], fp=fp: e.tensor_tensor(out=u.h[:, fp, 0:wd], in0=G.h[:, 0:wd], in1=c.h[:, 0:wd], op=ALU.mult),
                      r=[G, cs[1]], w=[u])
            kb.dma(POOL, UT_d.h[:, t0:t0 + wd].rearrange("(k p) t -> p k t", p=128), u.h[:, :, 0:wd], r=[u], w=[UT_d], owner=u)
        kb.stage_end()
        return nxt_slot

    def stage_sgu(W, pre=None):
        kb.stage_begin()
        nxt_slot = pre() if pre else None
        ws = kb.sb("ws", [128, 8, 128], BF16)
        buT = kb.sb("buT", [128, 16], F32)
        bvb = kb.sb("bvb", [1, 2 * D], BF16)
        vg = kb.sb("vg", [128, 2 * D], F32)
        vb = kb.sb("vb", [128, 2 * D], F32)
        bs2 = kb.sb("bs2", [2, 1024], F32)
        bsh = kb.sb("bsh", [2, 1024], BF16)
        bsr = kb.sb("bsr", [2, 1024], F32)
        bsl = kb.sb("bsl", [2, 1024], BF16)
        bshl = kb.sb("bshl", [2, 1024], BF16)
        s2 = kb.sb("s2", [2, 2], F32)
        hTs = [kb.sb(f"hs{i}", [128, KC, 512], BF16) for i in range(2)]
        uTs = [kb.sb(f"uT{i}", [128, 16, 512], BF16) for i in range(2)]
        gvs = [kb.sb(f"gv{i}", [128, 2 * D], F32) for i in range(3)]
        vlns = [kb.sb(f"vln{i}", [128, 2 * D], BF16) for i in range(2)]
        st6s = [kb.sb(f"st6{i}", [128, 24], F32) for i in range(3)]
        mvs = [kb.sb(f"mv{i}", [128, 2], F32) for i in range(3)]
        rstds = [kb.sb(f"rstdv{i}", [128, 2], F32) for i in range(3)]
        bi = 0
        yT = [kb.sb("yT0", [128, 16, 512], BF16)] * 2
        pu = [PV(0, 512, "pu0"), PV(512, 512, "pu1")]
        pv = PV(1024, 2048, "pv")
        pm = [PV(3072, 512, "pm0"), PV(3584, 512, "pm1")]
        kb.dma(POOL, ws.h[:], sgu_wsT.h[:, :, :], r=[sgu_wsT], w=[ws], owner=ws)
        kb.op(POOL, lambda e: e.memset(ws.h[64:128, :, 0:64], 0.0), w=[ws])
        kb.dma(SP, buT.h[:], sgu_buT.h[:, :], r=[sgu_buT], w=[buT], owner=buT)
        kb.dma(POOL, bvb.h[:], sgu_bv.h[:, :], r=[sgu_bv], w=[bvb], owner=bvb)
        kb.dma(SP, vg.h[:], sgu_vg.h[:, :], r=[sgu_vg], w=[vg], owner=vg)
        kb.dma(SP, vb.h[:], sgu_vb.h[:, :], r=[sgu_vb], w=[vb], owner=vb)
        kb.dma(SP, bs2.h[:], sgu_bs2.h[:, :], r=[sgu_bs2], w=[bs2], owner=bs2)
        kb.dma(SP, s2.h[:], sel2.h[:, :], r=[sel2], w=[s2], owner=s2)
        kb.op(DVE, lambda e: e.tensor_copy(out=bsh.h[:], in_=bs2.h[:]), r=[bs2], w=[bsh])
        kb.op(DVE, lambda e: e.tensor_tensor(out=bsr.h[:], in0=bs2.h[:], in1=bsh.h[:], op=ALU.subtract), r=[bs2, bsh], w=[bsr])
        kb.op(DVE, lambda e: e.tensor_copy(out=bsl.h[:], in_=bsr.h[:]), r=[bsr], w=[bsl])
        kb.op(DVE, lambda e: e.tensor_scalar(out=bsr.h[:], in0=bsl.h[:], scalar1=s2.h[:, 1:2], scalar2=None, op0=ALU.mult), r=[bsl, s2], w=[bsr])
        kb.op(DVE, lambda e: e.scalar_tensor_tensor(out=bshl.h[:], in0=bsh.h[:], scalar=s2.h[:, 0:1], in1=bsr.h[:], op0=ALU.mult, op1=ALU.add),
              r=[bsh, s2, bsr], w=[bshl])
        blocks = [(ti, j) for ti, (t0, wd) in enumerate(TILES) for j in range(wd // 128)]
        pui = [0]
        pmi = [0]

        def u_chunks(ti, fcs):
            t0, wd = TILES[ti]
            hT = hTs[ti % 2]
            for fc in fcs:
                p = pu[pui[0] % 2]
                pui[0] += 1
                kb.mm([lambda e, k=k, fc=fc, p=p: e.matmul(p.ap(c1=wd), lhsT=W.h[:, k, fc * 128:(fc + 1) * 128], rhs=hT.h[:, k, 0:wd],
                                                          start=(k == 0), stop=(k == KC - 1)) for k in range(KC)], r=[W, hT], w=[p.t])
                kb.op(ACT, lambda e, fc=fc, p=p: e.activation(out=uTs[ti % 2].h[:, fc, 0:wd], in_=p.ap(c1=wd), func=AF.Gelu_apprx_tanh,
                                                              bias=buT.h[:, fc:fc + 1]), r=[p.t, buT], w=[uTs[ti % 2]])

        def load_tile(ti):
            t0, wd = TILES[ti]
            hT = hTs[ti % 2]
            kb.dma(SP, hT.h[:, :, 0:wd], HT_d.h[:, t0:t0 + wd].rearrange("(k p) t -> p k t", p=128), r=[HT_d], w=[hT], owner=hT)

        def pe_v(bidx):
            ti, j = blocks[bidx]
            hT = hTs[ti % 2]
            fns = []
            for q4 in range(4):
                fns += [lambda e, k=k, q4=q4: e.matmul(pv.ap(c0=q4 * 512, c1=(q4 + 1) * 512), lhsT=hT.h[:, k, j * 128:(j + 1) * 128],
                                                      rhs=W.h[:, k, 2 * D + q4 * 512:2 * D + (q4 + 1) * 512], start=(k == 0), stop=False)
                        for k in range(KC)]
                fns.append(lambda e, q4=q4: e.matmul(pv.ap(c0=q4 * 512, c1=(q4 + 1) * 512), lhsT=onesb.h[0:1, 0:128],
                                                     rhs=bvb.h[0:1, q4 * 512:(q4 + 1) * 512], start=False, stop=True))
            kb.mm(fns, r=[W, hT, onesb, bvb], w=[pv.t])

        def act_gelu(bidx):
            gv = gvs[bidx % 3]
            kb.op(ACT, lambda e: e.activation(out=gv.h[:], in_=pv.ap(), func=AF.Gelu_apprx_tanh), r=[pv.t], w=[gv])

        def dve_stats(bidx):
            gv, st6, mv = gvs[bidx % 3], st6s[bidx % 3], mvs[bidx % 3]
            for c in range(4):
                kb.op(DVE, lambda e, c=c: e.bn_stats(out=st6.h[:, c * 6:(c + 1) * 6], in_=gv.h[:, c * 512:(c + 1) * 512]), r=[gv], w=[st6])
            kb.op(DVE, lambda e: e.bn_aggr(out=mv.h[:], in_=st6.h[:]), r=[st6], w=[mv])

        def norm_chain(bidx):
            gv, mv, rstd = gvs[bidx % 3], mvs[bidx % 3], rstds[bidx % 3]
            kb.op(ACT, lambda e: e.activation(out=rstd.h[:, 0:1], in_=mv.h[:, 1:2], func=AF.Ln, bias=cst.h[:, 0:1]), r=[mv, cst], w=[rstd])
            kb.op(ACT, lambda e: e.activation(out=rstd.h[:, 0:1], in_=rstd.h[:, 0:1], func=AF.Exp, scale=-0.5), r=[rstd], w=[rstd])
            kb.op(DVE, lambda e: e.scalar_tensor_tensor(out=rstd.h[:, 1:2], in0=mv.h[:, 0:1], scalar=-1.0, in1=rstd.h[:, 0:1],
                                                        op0=ALU.mult, op1=ALU.mult), r=[mv, rstd], w=[rstd])
            kb.op(ACT, lambda e: e.activation(out=gv.h[:], in_=gv.h[:], func=AF.Identity, scale=rstd.h[:, 0:1], bias=rstd.h[:, 1:2]),
                  r=[gv, rstd], w=[gv])

        def scale_bias(bidx):
            gv, vln = gvs[bidx % 3], vlns[bidx % 2]
            kb.op(DVE, lambda e: e.tensor_tensor(out=gv.h[:], in0=gv.h[:], in1=vg.h[:], op=ALU.mult), r=[gv, vg], w=[gv])
            kb.op(POOL, lambda e: e.tensor_tensor(out=vln.h[:], in0=gv.h[:], in1=vb.h[:], op=ALU.add), r=[gv, vb], w=[vln])

        def mix_y(bidx):
            ti, j = blocks[bidx]
            t0, wd = TILES[ti]
            nb = wd // 128
            y = yT[ti % 2]
            uT = uTs[ti % 2]
            vln = vlns[bidx % 2]
            for c4 in range(4):
                p = pm[pmi[0] % 2]
                pmi[0] += 1
                fns = []
                for cc in range(4):
                    fc = c4 * 4 + cc
                    g = fc // 2
                    fns.append(lambda e, fc=fc, g=g, cc=cc, p=p: e.matmul(p.ap(c0=cc * 128, c1=(cc + 1) * 128), lhsT=vln.h[:, fc * 128:(fc + 1) * 128],
                                                                         rhs=ws.h[:, g, :], start=True, stop=False, skip_group_check=True))
                    fns.append(lambda e, g=g, cc=cc, p=p: e.matmul(p.ap(c0=cc * 128, c1=(cc + 1) * 128), lhsT=onesb.h[0:2, 0:128],
                                                                  rhs=bshl.h[0:2, g * 128:(g + 1) * 128], start=False, stop=True, skip_group_check=True))
                kb.mm(fns, r=[vln, ws, onesb, bshl], w=[p.t])
                kb.op(DVE, lambda e, c4=c4, p=p: e.tensor_tensor(
                    out=y.h[:, c4 * 4:(c4 + 1) * 4, j * 128:(j + 1) * 128],
                    in0=p.ap().rearrange("p (c t) -> p c t", c=4),
                    in1=uT.h[:, c4 * 4:(c4 + 1) * 4, j * 128:(j + 1) * 128], op=ALU.mult), r=[p.t, uT], w=[y])
            if j == nb - 1:
                kb.dma(POOL, YT_d.h[:, t0:t0 + wd].rearrange("(k p) t -> p k t", p=128), y.h[:, :, 0:wd], r=[y], w=[YT_d], owner=y)

        load_tile(0)
        u_chunks(0, range(16))
        load_tile(1)
        NB = len(blocks)
        pe_v(0)
        act_gelu(0)
        dve_stats(0)
        pe_v(1)
        act_gelu(1)
        dve_stats(1)
        for bidx in range(NB):
            ti, j = blocks[bidx]
            nb = TILES[ti][1] // 128
            if bidx + 2 < NB:
                pe_v(bidx + 2)
            norm_chain(bidx)
            if bidx + 2 < NB:
                act_gelu(bidx + 2)
            scale_bias(bidx)
            if bidx + 2 < NB:
                dve_stats(bidx + 2)
            if ti + 1 < len(TILES):
                per = 16 // nb
                u_chunks(ti + 1, range(j * per, (j + 1) * per))
            mix_y(bidx)
            if j == nb - 1 and ti + 2 < len(TILES):
                load_tile(ti + 2)
        kb.stage_end()
        return nxt_slot

    def dump(name, t, shape):
        if name in kb.taps:
            d = kb.dram("dbg_" + name, shape, F32, kind="ExternalOutput")
            kb.dma(SP, d.h, t.h[:], r=[t], w=[d], owner=t)

    def run_all():
        w_in = prefetch_w("in", KC, 4 * D + H, "left", cast_src=fox_w_in.h)
        stage_adaln()
        if upto < 1:
            return
        stage_proj0(w_in[0])
        free_w(w_in)
        stage_decay()
        if upto < 2:
            return
        s_out = stage_attn(pre=lambda: prefetch_w("out", 8, D, "right", cast_src=fox_w_out.h))
        if upto < 3:
            return
        s_up0 = stage_out("a", AT_d, 8, s_out[0], 0, xv.h[E0:S, :], xv, XA_d, (0, 1), pre=lambda: prefetch_w("up0", KC, 2 * DFF, "left", defer=True))
        free_w(s_out)
        s_dn0 = stage_ffn_up(0, s_up0[0], pre=lambda: prefetch_w("dn0", 22, D, "right", defer=True))
        free_w(s_up0)
        stage_out("b", UT_d, 22, s_dn0[0], 1, XA_d.h, XA_d, XB_d, (1, 0))
        free_w(s_dn0)
        s_sgi = prefetch_w("sgi", KC, 4 * D, "left")
        stage_sgu(s_sgi[0])
        free_w(s_sgi)
        s_sgo = prefetch_w("sgo", 16, D, "right")
        s_up1 = stage_out("c", YT_d, 16, s_sgo[0], 2, XB_d.h, XB_d, XA_d, (1, 1), pre=lambda: prefetch_w("up1", KC, 2 * DFF, "left", defer=True), in_bufs=1)
        free_w(s_sgo)
        s_dn1 = stage_ffn_up(1, s_up1[0], pre=lambda: prefetch_w("dn1", 22, D, "right", defer=True))
        free_w(s_up1)
        stage_out("d", UT_d, 22, s_dn1[0], 3, XA_d.h, XA_d, None, None, final=True)
        free_w(s_dn1)

    run_all()
    kb.barrier()
    return kb


def make_in_maps(inp):
    f = lambda a: np.ascontiguousarray(a, dtype=np.float32)
    x = inp["x"]
    maps = []
    maskb = np.where(np.arange(128)[:, None] > np.arange(128)[None, :], NEG, 0.0).astype(np.float32)
    ngT = np.stack([inp["norm1_g"][0], inp["norm2_g"][0], inp["norm1_g"][1], inp["norm2_g"][1]])
    ngT = f(ngT.reshape(4, KC, 128).transpose(2, 0, 1).reshape(128, 4 * KC))
    ada_b = inp["ada_b"]
    ada_bT = f(ada_b.reshape(2, 48, 128).transpose(2, 0, 1).reshape(128, 96))
    bg = np.concatenate([ada_b[l, (2 + 3 * w) * D:(3 + 3 * w) * D] for l in range(2) for w in range(2)])
    ada_bg = f(np.broadcast_to(bg[None, :], (128, 4 * D)))
    cw = np.concatenate([inp["ffn_conv_w"], inp["ffn_conv_b"][:, None, :]], axis=1)
    ffn_cw = f(cw.reshape(2, 4, NFC, 128).transpose(0, 3, 1, 2))
    shared = {
        "maskb": maskb,
        "ada_w": f(inp["ada_w"]), "ada_bT": ada_bT, "ada_bg": ada_bg, "ngT": ngT,
        "final_g": f(np.broadcast_to(inp["final_g"][None, :], (128, D))),
        "fox_w_in": f(inp["fox_w_in"][0]), "fox_bf": f(inp["fox_b_f"][0].reshape(H, 1)),
        "fox_qg": f(np.tile(inp["fox_q_gain"][0], 2).reshape(128, 1)),
        "fox_kg": f(np.tile(inp["fox_k_gain"][0], 2).reshape(128, 1)),
        "fox_w_out": f(inp["fox_w_out"][0]),
        "sgu_w_in": f(inp["sgu_w_in"][0]),
        "sgu_buT": f(inp["sgu_b_in"][0][:2 * D].reshape(16, 128).T),
        "sgu_bv": f(inp["sgu_b_in"][0][2 * D:].reshape(1, 2 * D)),
        "sgu_vg": f(np.broadcast_to(inp["sgu_v_gain"][0][None, :], (128, 2 * D))),
        "sgu_vb": f(np.broadcast_to(inp["sgu_v_bias"][0][None, :], (128, 2 * D))),
        "sgu_wsT": f(inp["sgu_w_s"][0].transpose(2, 0, 1)),
        "sgu_bs2": f(np.broadcast_to(inp["sgu_b_s"][0].reshape(1, 8 * 128), (2, 1024))),
        "sel2": np.eye(2, dtype=np.float32),
        "sgu_w_out": f(inp["sgu_w_out"][0]),
        "ffn_w_up": f(inp["ffn_w_up"]), "ffn_cw": ffn_cw, "ffn_w_down": f(inp["ffn_w_down"]),
    }
    for c in range(8):
        b, half = c // 2, c % 2
        if half == 1:
            xvv = f(x[b])
            kp = np.zeros((H, S), np.float32)
            tm = np.ones((128, HALO), np.float32)
        else:
            xvv = np.zeros((S, D), np.float32)
            xvv[OWN:] = x[b, :OWN]
            kp = np.zeros((H, S), np.float32)
            kp[:, :OWN] = NEG
            tm = np.zeros((128, HALO), np.float32)
        m = dict(shared)
        m.update({"xv": xvv, "cvec": f(inp["c"][b].reshape(KC, 128).T), "kpad": kp, "tmask": tm})
        maps.append(m)
    return maps


def kernel(**inputs):
    inp = {k: np.asarray(v) for k, v in inputs.items()}
    kb = build()
    maps = make_in_maps(inp)
    res = run_bass_kernel_spmd(kb.nc, maps, core_ids=list(range(8)))
    out = np.zeros((4, S, D), np.float32)
    for c in range(8):
        b, half = c // 2, c % 2
        out[b, half * OWN:(half + 1) * OWN] = res.results[c]["out"]
    return out
```
